# Optimizing a Trainium2 kernel written in Bass

```python
import jax, jax.numpy as jnp
from jax import lax
import numpy as np

D_MODEL = 2048
BATCH = 1
SEQ = 16384
DEPTH = 1

D_CONV = D_MODEL // 2
CONV_WIDTH = 3
N_HEADS = 16
N_KV_HEADS = 4
HEAD_DIM = 64
D_ATTN = N_HEADS * HEAD_DIM
D_KV = N_KV_HEADS * HEAD_DIM
WINDOW = 128
BLOCK = 128
ROPE_THETA = 10000.0
N_BRANCH = 2
D_FF = 5632
EPS = 1e-6

SPLIT_SIZES = (D_CONV, D_CONV, D_CONV, D_ATTN, D_KV, D_KV, D_MODEL, D_MODEL)
SPLIT_POINTS = tuple(int(p) for p in np.cumsum(SPLIT_SIZES)[:-1])
D_IN_PROJ = int(sum(SPLIT_SIZES))

kernel_name = "hybrid_gated_shortconv_swa_convffn"


def rmsnorm(x, g):
    xf = x.astype(jnp.float32)
    y = xf * lax.rsqrt(jnp.mean(xf * xf, axis=-1, keepdims=True) + EPS)
    return (y * g.astype(jnp.float32)).astype(x.dtype)


def centred_dwconv3(x, w):
    xp = jnp.pad(x, ((0, 0), (1, 1), (0, 0)))
    return xp[:, :-2] * w[0] + xp[:, 1:-1] * w[1] + xp[:, 2:] * w[2]


def rope(x, seq_len):
    half = HEAD_DIM // 2
    inv_freq = ROPE_THETA ** (-jnp.arange(0, half, dtype=jnp.float32) / half)
    ang = jnp.arange(seq_len, dtype=jnp.float32)[:, None] * inv_freq[None, :]
    cos = jnp.cos(ang)[None, :, None, :]
    sin = jnp.sin(ang)[None, :, None, :]
    xf = x.astype(jnp.float32)
    x1, x2 = xf[..., :half], xf[..., half:]
    out = jnp.concatenate([x1 * cos - x2 * sin, x2 * cos + x1 * sin], axis=-1)
    return out.astype(x.dtype)


def banded_window_attention(q, k, v, sink):
    b, s = q.shape[0], q.shape[1]
    nb = s // BLOCK
    grp = N_HEADS // N_KV_HEADS
    qb = q.reshape(b, nb, BLOCK, N_KV_HEADS, grp, HEAD_DIM)
    pad = ((0, 0), (BLOCK, BLOCK), (0, 0), (0, 0))
    kp = jnp.pad(k, pad).reshape(b, nb + 2, BLOCK, N_KV_HEADS, HEAD_DIM)
    vp = jnp.pad(v, pad).reshape(b, nb + 2, BLOCK, N_KV_HEADS, HEAD_DIM)
    kb = jnp.concatenate([kp[:, :-2], kp[:, 1:-1], kp[:, 2:]], axis=2)
    vb = jnp.concatenate([vp[:, :-2], vp[:, 1:-1], vp[:, 2:]], axis=2)
    scale = HEAD_DIM ** -0.5
    scores = jnp.einsum('bnqkgd,bnskd->bnkgqs', qb, kb).astype(jnp.float32) * scale
    blk = jnp.arange(nb)[:, None, None]
    qpos = blk * BLOCK + jnp.arange(BLOCK)[None, :, None]
    kpos = (blk - 1) * BLOCK + jnp.arange(3 * BLOCK)[None, None, :]
    mask = (jnp.abs(kpos - qpos) <= WINDOW) & (kpos >= 0) & (kpos < s)
    scores = jnp.where(mask[None, :, None, None], scores, -jnp.inf)
    sk = sink.astype(jnp.float32).reshape(1, 1, N_KV_HEADS, grp, 1)
    m = jnp.maximum(jnp.max(scores, axis=-1), sk)
    p = jnp.exp(scores - m[..., None])
    denom = jnp.sum(p, axis=-1) + jnp.exp(sk - m)
    p = (p / denom[..., None]).astype(v.dtype)
    out = jnp.einsum('bnkgqs,bnskd->bnqkgd', p, vb)
    return out.reshape(b, s, N_HEADS, HEAD_DIM)


def setup_inputs(seed: int = 0) -> dict:
    key = jax.random.key(seed)
    ks = jax.random.split(key, 16)
    f32 = jnp.float32

    def nrm(k, shape, fan_in):
        return jax.random.normal(k, shape, f32) * (fan_in ** -0.5)

    x = jax.random.normal(ks[0], (BATCH, SEQ, D_MODEL), f32)
    return {
        "x": x,
        "norm_mix_g": 1.0 + 0.02 * jax.random.normal(ks[1], (DEPTH, D_MODEL), f32),
        "w_in": nrm(ks[2], (DEPTH, D_MODEL, D_IN_PROJ), D_MODEL),
        "b_gate": 0.02 * jax.random.normal(ks[3], (DEPTH, N_BRANCH * D_MODEL), f32),
        "conv_a_w": nrm(ks[4], (DEPTH, CONV_WIDTH, D_CONV), CONV_WIDTH),
        "w_out_a": nrm(ks[5], (DEPTH, D_CONV, D_MODEL), D_CONV),
        "sink_logits": 0.5 * jax.random.normal(ks[6], (DEPTH, N_HEADS), f32),
        "w_o_attn": nrm(ks[7], (DEPTH, D_ATTN, D_MODEL), D_ATTN),
        "w_mix_out": nrm(ks[8], (DEPTH, D_MODEL, D_MODEL), D_MODEL),
        "norm_ffn_g": 1.0 + 0.02 * jax.random.normal(ks[9], (DEPTH, D_MODEL), f32),
        "ffn_w_up": nrm(ks[10], (DEPTH, D_MODEL, 2 * D_FF), D_MODEL),
        "ffn_conv_w": nrm(ks[11], (DEPTH, CONV_WIDTH, 2 * D_FF), CONV_WIDTH),
        "ffn_conv_b": 0.02 * jax.random.normal(ks[12], (DEPTH, 2 * D_FF), f32),
        "ffn_w_down": nrm(ks[13], (DEPTH, D_FF, D_MODEL), D_FF),
        "norm_final_g": 1.0 + 0.02 * jax.random.normal(ks[14], (D_MODEL,), f32),
    }


def reference(x, norm_mix_g, w_in, b_gate, conv_a_w, w_out_a, sink_logits, w_o_attn,
              w_mix_out, norm_ffn_g, ffn_w_up, ffn_conv_w, ffn_conv_b, ffn_w_down,
              norm_final_g):
    b, s, _ = x.shape
    h = x
    for l in range(DEPTH):
        u = rmsnorm(h, norm_mix_g[l])
        z = u @ w_in[l]
        b_a, c_a, v_a, q, k, v, gl_a, gl_b = jnp.split(z, SPLIT_POINTS, axis=-1)
        y_a = (b_a * centred_dwconv3(c_a * v_a, conv_a_w[l])) @ w_out_a[l]
        q = rope(q.reshape(b, s, N_HEADS, HEAD_DIM), s)
        k = rope(k.reshape(b, s, N_KV_HEADS, HEAD_DIM), s)
        v = v.reshape(b, s, N_KV_HEADS, HEAD_DIM)
        att = banded_window_attention(q, k, v, sink_logits[l])
        y_b = att.reshape(b, s, D_ATTN) @ w_o_attn[l]
        g_a = jax.nn.sigmoid(gl_a + b_gate[l, :D_MODEL])
        g_b = jax.nn.sigmoid(gl_b + b_gate[l, D_MODEL:])
        h = h + (g_a * y_a + g_b * y_b) @ w_mix_out[l]
        u2 = rmsnorm(h, norm_ffn_g[l])
        up = centred_dwconv3(u2 @ ffn_w_up[l], ffn_conv_w[l]) + ffn_conv_b[l]
        a, gv = up[..., :D_FF], up[..., D_FF:]
        h = h + (jax.nn.silu(a) * gv) @ ffn_w_down[l]
    return rmsnorm(h, norm_final_g)
```

```python
import numpy as np
from contextlib import ExitStack

import concourse.bass as bass
import concourse.mybir as mybir
from concourse.bass_utils import run_bass_kernel_spmd

F32 = mybir.dt.float32
BF16 = mybir.dt.bfloat16
AF = mybir.ActivationFunctionType
ALU = mybir.AluOpType

D = 2048
SEQ = 16384
NCORE = 8
TOKC = SEQ // NCORE
T = 1024
NT = TOKC // T
HALO = 256
XROWS = TOKC + 2 * HALO
KVN = T + 2 * HALO
R0 = 254
RN = 1028
NKB = KVN // 128
D_CONV = 1024
NH = 16
NKV = 4
HD = 64
D_FF = 5632
NFF = D_FF // 128
EPS = 1e-6
FN = T + 2
GRP = 4
NGRP = NFF // GRP

DEBUG = False
MASK_PE = True
PIPE7 = True
PIPE9 = True
FFN_XB = False

PC_CONVA = 0
PC_BGATE = PC_CONVA + 24
PC_FCW = PC_BGATE + 32
PC_FCB = PC_FCW + 264
PC_SINK = PC_FCB + 88
NPAR = PC_SINK + 16


class View:
    __slots__ = ("ap", "space", "p0", "p1", "b0", "b1", "ivs")

    def __init__(self, ap, space, p0, p1, b0, b1, ivs=None):
        self.ap = ap
        self.space = space
        self.p0, self.p1, self.b0, self.b1 = p0, p1, b0, b1
        self.ivs = ivs


def _ivs_overlap(a, b):
    for (x0, x1) in a:
        for (y0, y1) in b:
            if x0 < y1 and y0 < x1:
                return True
    return False


class Buf:
    def __init__(self, arena, space, off, dtype, shape):
        self.arena = arena
        self.space = space
        self.off = off
        self.dtype = dtype
        self.esz = 2 if dtype == BF16 else 4
        self.shape = tuple(shape)
        n = 1
        for s in shape:
            n *= s
        self.nelem = n
        self.nbytes = n * self.esz
        assert off % 4 == 0 and self.nbytes % 4 == 0, (off, shape)
        st = []
        acc = 1
        for s in reversed(shape):
            st.append(acc)
            acc *= s
        self.strides = tuple(reversed(st))

    def end(self):
        return self.off + self.nbytes

    def __call__(self, *idx, p=(0, 128)):
        p0, p1 = p
        base = self.arena[p0:p1, self.off // 4:(self.off + self.nbytes) // 4]
        if self.dtype == BF16:
            base = base.bitcast(BF16)
        if len(self.shape) == 2:
            base = base.rearrange("p (a b) -> p a b", a=self.shape[0])
        elif len(self.shape) == 3:
            base = base.rearrange("p (a b c) -> p a b c", a=self.shape[0], b=self.shape[1])
        elif len(self.shape) == 4:
            base = base.rearrange("p (a b c d) -> p a b c d", a=self.shape[0], b=self.shape[1], c=self.shape[2])
        idx = list(idx) + [None] * (len(self.shape) - len(idx))
        key = [slice(None)]
        lo = 0
        hi = 0
        for d, ix in enumerate(idx):
            n = self.shape[d]
            if ix is None:
                key.append(slice(None))
                hi += (n - 1) * self.strides[d]
            elif isinstance(ix, int):
                assert 0 <= ix < n, (ix, n, self.shape)
                key.append(ix)
                lo += ix * self.strides[d]
                hi += ix * self.strides[d]
            else:
                a, b = ix[0], ix[1]
                step = ix[2] if len(ix) > 2 else 1
                assert 0 <= a < b <= n, (ix, n, self.shape)
                key.append(slice(a, b, step) if step != 1 else slice(a, b))
                last = a + ((b - 1 - a) // step) * step
                lo += a * self.strides[d]
                hi += last * self.strides[d]
        ap = base[tuple(key)]
        ivs = None
        if len(self.shape) >= 2:
            rngs = []
            for d, ix in enumerate(idx):
                n = self.shape[d]
                if ix is None:
                    rngs.append((0, n, 1))
                elif isinstance(ix, int):
                    rngs.append((ix, ix + 1, 1))
                else:
                    rngs.append((ix[0], ix[1], ix[2] if len(ix) > 2 else 1))
            a_, b_, st_ = rngs[-1]
            if st_ == 1:
                inner = [(a_, b_)]
            else:
                inner = [(j, j + 1) for j in range(a_, b_, st_)]
            outer = [0]
            for d in range(len(self.shape) - 1):
                a2, b2, st2 = rngs[d]
                outer = [o + j * self.strides[d] for o in outer for j in range(a2, b2, st2)]
            if len(outer) * len(inner) <= 64:
                raw = sorted((o + i0, o + i1) for o in outer for (i0, i1) in inner)
                merged = []
                for (x0, x1) in raw:
                    if merged and x0 <= merged[-1][1]:
                        merged[-1][1] = max(merged[-1][1], x1)
                    else:
                        merged.append([x0, x1])
                if len(merged) > 1:
                    ivs = [(self.off + m0 * self.esz, self.off + m1 * self.esz) for (m0, m1) in merged]
        return View(ap, self.space, p0, p1, self.off + lo * self.esz, self.off + (hi + 1) * self.esz, ivs)


class Op:
    __slots__ = ("eng", "fn", "reads", "writes", "dma_sem", "dma_cnt", "deps", "idx", "ms", "group")

    def __init__(self, eng, fn, reads, writes, dma_sem=None, group=False):
        self.eng = eng
        self.fn = fn
        self.reads = reads
        self.writes = writes
        self.dma_sem = dma_sem
        self.dma_cnt = 0
        self.deps = []
        self.idx = -1
        self.ms = 0
        self.group = group


class Prog:
    ENGS = ("pe", "act", "dve", "pool", "sp")

    def __init__(self):
        self.ops = []
        self.recs = {"sb": [], "ps": []}
        self.dma_counts = {}
        self.final_waits = []

    def add(self, eng, fn, reads=(), writes=(), dma_sem=None, group=False):
        op = Op(eng, fn, [v for v in reads if v is not None], [v for v in writes if v is not None], dma_sem, group)
        op.idx = len(self.ops)
        if dma_sem is not None:
            self.dma_counts[dma_sem] = self.dma_counts.get(dma_sem, 0) + 16
            op.dma_cnt = self.dma_counts[dma_sem]
        deps = set()
        ops = self.ops
        pe_self = (eng == "pe")
        is_dma = dma_sem is not None
        for v in op.reads:
            for r in self.recs[v.space]:
                if r[4] == "w" and r[0] < v.p1 and v.p0 < r[1] and r[2] < v.b1 and v.b0 < r[3]:
                    if (v.ivs is not None or r[6] is not None) and not _ivs_overlap(v.ivs or [(v.b0, v.b1)], r[6] or [(r[2], r[3])]):
                        continue
                    deps.add(r[5])
        for v in op.writes:
            for r in self.recs[v.space]:
                if r[0] < v.p1 and v.p0 < r[1] and r[2] < v.b1 and v.b0 < r[3]:
                    if (v.ivs is not None or r[6] is not None) and not _ivs_overlap(v.ivs or [(v.b0, v.b1)], r[6] or [(r[2], r[3])]):
                        continue
                    deps.add(r[5])
        if pe_self:
            deps = {d for d in deps if not (ops[d].eng == "pe" and ops[d].dma_sem is None)}
        op.deps = sorted(deps)
        for v in op.writes:
            lst = self.recs[v.space]
            if v.ivs is None:
                lst[:] = [r for r in lst if not (v.p0 <= r[0] and r[1] <= v.p1 and v.b0 <= r[2] and r[3] <= v.b1)]
            else:
                lst[:] = [r for r in lst if not (v.p0 <= r[0] and r[1] <= v.p1 and r[6] == v.ivs)]
            lst.append([v.p0, v.p1, v.b0, v.b1, "w", op.idx, v.ivs])
        for v in op.reads:
            lst = self.recs[v.space]
            done = False
            if not is_dma:
                for r in lst:
                    if r[4] == "r" and r[0] == v.p0 and r[1] == v.p1 and r[2] == v.b0 and r[3] == v.b1 and r[6] == v.ivs:
                        o = ops[r[5]]
                        if o.eng == eng and o.dma_sem is None:
                            r[5] = op.idx
                            done = True
                            break
            if not done:
                lst.append([v.p0, v.p1, v.b0, v.b1, "r", op.idx, v.ivs])
        self.ops.append(op)
        return op

    def emit(self, nc, es):
        ops = self.ops
        needed = set()
        for op in ops:
            for d in op.deps:
                needed.add(d)
        cnt = {e: 0 for e in self.ENGS}
        for op in ops:
            if op.dma_sem is None and op.idx in needed:
                cnt[op.eng] += 1
                op.ms = cnt[op.eng]
        esem = {e: es.enter_context(nc.semaphore("ms_" + e)) for e in self.ENGS}
        dsem = {k: es.enter_context(nc.semaphore("dq_" + k)) for k in self.dma_counts}
        block = es.enter_context(nc.Block())
        by_eng = {e: [op for op in ops if op.eng == e] for e in self.ENGS}
        final_total = dict(self.dma_counts)
        out_sems = list(self.final_waits)

        def run(engname, eng):
            waited = {}
            for op in by_eng[engname]:
                for d in op.deps:
                    o = ops[d]
                    if o.dma_sem is not None:
                        sem = dsem[o.dma_sem]
                        val = final_total[o.dma_sem] if o.group else o.dma_cnt
                        key = "d" + o.dma_sem
                    else:
                        sem = esem[o.eng]
                        val = o.ms
                        key = "e" + o.eng
                    if waited.get(key, 0) >= val:
                        continue
                    waited[key] = val
                    eng.wait_ge(sem, val)
                ins = op.fn(eng)
                if op.dma_sem is not None:
                    ins.then_inc(dsem[op.dma_sem], 16)
                elif op.ms:
                    ins.then_inc(esem[op.eng], 1)
            if engname == "sp":
                for k in out_sems:
                    eng.wait_ge(dsem[k], final_total[k])

        block.tensor(lambda e: run("pe", e))
        block.scalar(lambda e: run("act", e))
        block.vector(lambda e: run("dve", e))
        block.gpsimd(lambda e: run("pool", e))
        block.sync(lambda e: run("sp", e))


def pieces(n, step=512):
    out = []
    a = 0
    while a < n:
        b = min(a + step, n)
        out.append((a, b))
        a = b
    if len(out) >= 2 and (out[-1][1] - out[-1][0]) <= 8:
        out = [out[-1]] + out[:-1]
    return out


def build_program(dbg=False):
    nc = bass.Bass("TRN2", target_bir_lowering=False)
    P = Prog()
    es = ExitStack()

    x_d = nc.dram_tensor("x", [XROWS, D], F32, kind="ExternalInput").ap()
    win_d = nc.dram_tensor("w_in_t", [34, 128, 4096], F32, kind="ExternalInput").ap()
    woo_d = nc.dram_tensor("w_oo_t", [8, 128, 4096], F32, kind="ExternalInput").ap()
    wmix_d = nc.dram_tensor("w_mix_t", [4, 128, 8192], F32, kind="ExternalInput").ap()
    wup_d = nc.dram_tensor("w_up_t", [2 * NGRP, 128, 8192], F32, kind="ExternalInput").ap()
    wdn_d = nc.dram_tensor("w_dn_t", [NGRP, 128, 8192], F32, kind="ExternalInput").ap()
    par_d = nc.dram_tensor("params", [128, NPAR], F32, kind="ExternalInput").ap()
    gvec_d = nc.dram_tensor("gvec", [3, D], F32, kind="ExternalInput").ap()
    rope_d = nc.dram_tensor("rope", [NT, 128, 2 * KVN], F32, kind="ExternalInput").ap()
    kbias_d = nc.dram_tensor("kbias", [NT, 128, NKB], F32, kind="ExternalInput").ap()
    eval_d = nc.dram_tensor("evalid", [NT, 2, 1], F32, kind="ExternalInput").ap()
    cmask_d = nc.dram_tensor("cmask", [128, 512], F32, kind="ExternalInput").ap()
    out_d = nc.dram_tensor("out", [TOKC, D], F32, kind="ExternalOutput").ap()
    dbg_d = {}
    if dbg:
        dbg_d["uT"] = nc.dram_tensor("dbg_uT", [128, 16 * KVN], BF16, kind="ExternalOutput").ap()
        dbg_d["ya"] = nc.dram_tensor("dbg_ya", [128, 8 * RN], BF16, kind="ExternalOutput").ap()
        dbg_d["qT"] = nc.dram_tensor("dbg_qT", [128, 8 * RN], BF16, kind="ExternalOutput").ap()
        dbg_d["kT"] = nc.dram_tensor("dbg_kT", [128, 4 * KVN], BF16, kind="ExternalOutput").ap()
        dbg_d["v"] = nc.dram_tensor("dbg_v", [128, NKB * 4 * 128], BF16, kind="ExternalOutput").ap()
        dbg_d["att"] = nc.dram_tensor("dbg_att", [128, 8 * RN], BF16, kind="ExternalOutput").ap()
        dbg_d["mT"] = nc.dram_tensor("dbg_mT", [128, 16 * RN], BF16, kind="ExternalOutput").ap()
        dbg_d["h1"] = nc.dram_tensor("dbg_h1", [128, 8 * D], F32, kind="ExternalOutput").ap()
        dbg_d["u2T"] = nc.dram_tensor("dbg_u2T", [128, 16 * RN], BF16, kind="ExternalOutput").ap()

    SB_BYTES = 206 * 1024
    sb = es.enter_context(nc.sbuf_tensor("arena", [128, SB_BYTES // 4], F32))
    ps = es.enter_context(nc.psum_tensor("psarena", [128, 4096], F32))

    cur = [0]

    def alloc(nbytes):
        o = cur[0]
        cur[0] += (nbytes + 31) // 32 * 32
        return o

    def SB(off, dtype, shape):
        return Buf(sb, "sb", off, dtype, shape)

    o_cm = alloc(512 * 2)
    cm = SB(o_cm, BF16, (512,))
    o_par = alloc(NPAR * 4)
    par = SB(o_par, F32, (NPAR,))
    o_esink = alloc(16 * 4)
    esink = SB(o_esink, F32, (16,))
    o_kb = alloc(NKB * 4)
    kbias = SB(o_kb, F32, (NKB,))
    o_ev = alloc(4)
    evalid = SB(o_ev, F32, (1,))
    o_ss = alloc(16 * 4)
    ssb = SB(o_ss, F32, (16,))
    o_rs = alloc(16 * 4)
    rsb = SB(o_rs, F32, (16,))
    o_C = alloc(16 * KVN * 2)
    uT = SB(o_C, BF16, (16, KVN))
    o_D = alloc(8 * RN * 2)
    ya = SB(o_D, BF16, (8, RN))
    assert cur[0] - o_C >= 8 * D * 4, (cur[0] - o_C)
    hbuf = SB(o_C, F32, (8, D))
    o_E = alloc(8 * RN * 2)
    att = SB(o_E, BF16, (8, RN))
    o_B = alloc(2 * KVN * 4)
    rope = SB(o_B, F32, (2, KVN))
    EB = cur[0] - o_E
    NXS = 3
    hx = SB(o_E, F32, (D,))
    u2n = [SB(o_E + 8192 + i * 4096, BF16, (D,)) for i in range(3)]
    g2bc = SB(o_E + 20480, F32, (D,))
    assert 20480 + 8192 <= EB, EB
    o_F = alloc(8 * RN * 2 + 4 * KVN * 2 + NKB * 4 * 128 * 2 + 64)
    qT = SB(o_F, BF16, (8, RN))
    kT = SB(o_F + 8 * RN * 2, BF16, (4, KVN))
    vA = SB(o_F + 8 * RN * 2 + 4 * KVN * 2, BF16, (NKB, 4, 128))
    g1bc = SB(o_F, F32, (D,))
    mT = SB(o_F, BF16, (16, RN))
    u2T = SB(o_F, BF16, (16, RN))
    assert 16 * RN * 2 + 8192 <= cur[0] - o_F
    xs = [SB(o_F + 8192 + i * 8192, F32, (D,)) for i in range(NXS)]
    xn = [SB(o_F + 8192 + NXS * 8192 + i * 4096, BF16, (D,)) for i in range(2)]
    assert 8192 + NXS * 8192 + 8192 <= cur[0] - o_F
    junk0 = SB(o_E + 24576, BF16, (D,))
    assert 24576 + 4096 <= EB
    wm0b = SB(o_F + 16 * RN * 2, BF16, (8, 512))
    o_G = alloc(25 * 1024)
    o_H = alloc(32 * 1024)
    o_I = alloc(8 * 1024)
    wm0a = SB(o_I, BF16, (8, 512))
    assert cur[0] <= SB_BYTES, cur[0]
    GHI = cur[0] - o_G
    TM = 4128
    tmpA = [SB(o_G + i * TM, F32, (RN,)) for i in range(6)]
    tmpK = [SB(o_G + i * 6144, F32, (KVN,)) for i in range(4)]
    den = SB(o_G, F32, (RN,))
    rden = SB(o_G + TM, F32, (RN,))
    ptile = [SB(o_G + 2 * TM + i * 768, BF16, (384,)) for i in range(3)]
    assert 2 * TM + 3 * 768 <= 25 * 1024
    ringM = [SB(o_H + i * 8192, BF16, (4096,)) for i in range(4)]
    ringF = [SB(o_G + i * 16384, BF16, (8192,)) for i in range(3)]
    o_act = o_G + 3 * 16384
    actT = [SB(o_act + i * 8192, BF16, (GRP, T)) for i in range(2)]
    assert o_act + 16384 <= cur[0]
    junk = SB(o_act + 8192, BF16, (D,))
    FT = 4128
    psb = [SB(o_E + i * FT, F32, (FN,)) for i in range(2)]
    a_c = [SB(o_E + 2 * FT + i * 4096, F32, (T,)) for i in range(2)]
    gv_c = [SB(o_E + 2 * FT + 8192 + i * 4096, F32, (T,)) for i in range(2)]
    sil = [SB(o_E + 2 * FT + 16384 + i * 4096, F32, (T,)) for i in range(1)]
    assert 2 * FT + 16384 + 4096 <= EB, (EB,)
    ost = [SB(o_E + i * 8192, F32, (D,)) for i in range(2)]
    gFbc = SB(o_E + 16384, F32, (D,))

    def PS(col, dtype, shape):
        return Buf(ps, "ps", col * 4, dtype, shape)

    PG = [PS(0, F32, (1536,)), PS(1536, F32, (1536,))]
    PGT = [PS(0, BF16, (16, 128)), PS(1536, BF16, (16, 128))]
    PB = [PS(3072, F32, (512,)), PS(3584, F32, (512,))]
    PB3 = [PS(2560, F32, (512,)), PS(3072, F32, (512,)), PS(3584, F32, (512,))]
    UG = [PS(0, F32, (1024,)), PS(1024, F32, (1024,))]
    XB = PS(2048, F32, (512,))
    DB4 = [PS(3072, F32, (512,)), PS(3584, F32, (512,)), PS(1024, F32, (512,)), PS(2560, F32, (512,))]
    db4_i = [0]

    def dma(q, out_v, in_ap, sem, group=False, reads=()):
        P.add(q, lambda e, o=out_v.ap, i=in_ap: e.dma_start(out=o, in_=i), reads=reads, writes=[out_v], dma_sem=sem, group=group)

    def dma_out(q, out_ap, in_v, sem):
        P.add(q, lambda e, o=out_ap, i=in_v.ap: e.dma_start(out=o, in_=i), reads=[in_v], writes=[], dma_sem=sem)
        if sem not in P.final_waits:
            P.final_waits.append(sem)

    def act(out_v, in_v, func, bias=None, scale=None, accum=None, extra_reads=()):
        kw = {}
        rd = [in_v] + list(extra_reads)
        wr = [out_v]
        if bias is not None:
            if isinstance(bias, View):
                kw["bias"] = bias.ap
                rd.append(bias)
            else:
                kw["bias"] = bias
        if scale is not None:
            if isinstance(scale, View):
                kw["scale"] = scale.ap
                rd.append(scale)
            else:
                kw["scale"] = scale
        if accum is not None:
            kw["accum_out"] = accum.ap
            wr.append(accum)
        P.add("act", lambda e, o=out_v.ap, i=in_v.ap, f=func, kw=kw: e.activation(out=o, in_=i, func=f, **kw), reads=rd, writes=wr)

    def tt(out_v, in0, in1, op, eng="dve"):
        P.add(eng, lambda e, o=out_v.ap, a=in0.ap, b=in1.ap, op=op: e.tensor_tensor(out=o, in0=a, in1=b, op=op), reads=[in0, in1], writes=[out_v])

    def stt(out_v, in0, scalar, in1, op0, op1, eng="dve"):
        rd = [in0, in1]
        s = scalar
        if isinstance(scalar, View):
            rd.append(scalar)
            s = scalar.ap
        P.add(eng, lambda e, o=out_v.ap, a=in0.ap, s=s, b=in1.ap, op0=op0, op1=op1: e.scalar_tensor_tensor(out=o, in0=a, scalar=s, in1=b, op0=op0, op1=op1), reads=rd, writes=[out_v])

    def ts(out_v, in0, s1, op0, eng="dve"):
        rd = [in0]
        s = s1
        if isinstance(s1, View):
            rd.append(s1)
            s = s1.ap
        P.add(eng, lambda e, o=out_v.ap, a=in0.ap, s=s, op0=op0: e.tensor_scalar(out=o, in0=a, scalar1=s, scalar2=None, op0=op0), reads=rd, writes=[out_v])

    def recip(out_v, in_v):
        P.add("dve", lambda e, o=out_v.ap, i=in_v.ap: e.reciprocal(out=o, in_=i), reads=[in_v], writes=[out_v])

    def copy(out_v, in_v, eng):
        if eng == "act":
            P.add("act", lambda e, o=out_v.ap, i=in_v.ap: e.activation(out=o, in_=i, func=AF.Copy), reads=[in_v], writes=[out_v])
        else:
            P.add(eng, lambda e, o=out_v.ap, i=in_v.ap: e.tensor_copy(out=o, in_=i), reads=[in_v], writes=[out_v])

    def memset(out_v, val, eng="dve"):
        P.add(eng, lambda e, o=out_v.ap, v=val: e.memset(o, v), reads=[], writes=[out_v])

    def mm(out_v, lhsT, rhs, start, stop, skip=False):
        P.add("pe", lambda e, o=out_v.ap, l=lhsT.ap, r=rhs.ap, s=start, t=stop, sk=skip: e.matmul(o, lhsT=l, rhs=r, start=s, stop=t, skip_group_check=sk), reads=[lhsT, rhs], writes=[out_v])

    def tr(out_v, in_v, ident_v):
        P.add("pe", lambda e, o=out_v.ap, i=in_v.ap, d=ident_v.ap: e.transpose(o, i, d), reads=[in_v, ident_v], writes=[out_v])

    ring_state = {"M": 0, "F": 0}

    def wload(kind, dram_ap, reads=()):
        if kind == "M":
            i = ring_state["M"] % 4
            ring_state["M"] += 1
            slot = ringM[i]
            sem = "rm%d" % i
        else:
            i = ring_state["F"] % 3
            ring_state["F"] += 1
            slot = ringF[i]
            sem = "rf%d" % i
        dma("pool", slot(), dram_ap, sem, reads=reads)
        return slot

    def wview(slot, shape):
        return Buf(sb, "sb", slot.off, BF16, shape)

    dma("pool", cm(), cmask_d[:, :], "c_pool", group=True)
    dma("sp", par(), par_d[:, :], "c_sp", group=True)
    ident = cm((0, 128))
    mask_le = lambda a, b: cm((128 + a, 128 + b))
    mask_ge = lambda a, b: cm((384 + a, 384 + b))
    act(esink(), par((PC_SINK, PC_SINK + 16)), AF.Exp)

    psg = [0]

    def next_pg():
        g = psg[0] % 2
        psg[0] += 1
        return g

    psb_i = [0]

    def next_pb():
        b = psb_i[0] % 2
        psb_i[0] += 1
        return b

    pb3_i = [0]

    def next_pb3():
        b = pb3_i[0] % 3
        pb3_i[0] += 1
        return b

    ug_i = [0]
    xb_i = [0]

    def proj(wv, nk, rhs_fn, ncols, grp):
        pcs = pieces(ncols)
        for k in range(nk):
            for (a, b) in pcs:
                mm(PG[grp]((a, b)), wv(k), rhs_fn(k, a, b), k == 0, k == nk - 1)

    xsem = ["xs%d" % i for i in range(NXS)]

    for ti in range(NT):
        row0 = ti * T
        dma("sp", kbias(), kbias_d[ti], "tab2")
        dma("sp", evalid(p=(0, 2)), eval_d[ti], "tab3")
        if ti == 0:
            dma("sp", g1bc(), gvec_d[0, :].partition_broadcast(128), "g1")
        wtiles = {}

        def prefetch_win(tno, reads=()):
            slot = wload("M", win_d[tno], reads=reads)
            wtiles[tno] = wview(slot, (16, 256))

        def p0_load(blk, q="sp", r0=None):
            sx = blk % NXS
            rr = row0 if r0 is None else r0
            dma(q, xs[sx](), x_d[rr + blk * 128: rr + (blk + 1) * 128, :], xsem[sx] if q == "sp" else "xp%d" % sx)

        def p0_act(blk):
            sx = blk % NXS
            c = blk % 16
            act(junk0(), xs[sx](), AF.Square, accum=ssb((c, c + 1)))
            act(rsb((c, c + 1)), ssb((c, c + 1)), AF.Sqrt, bias=EPS, scale=1.0 / D)

        def p0_dve(blk):
            sx = blk % NXS
            s = blk % 2
            c = blk % 16
            recip(rsb((c, c + 1)), rsb((c, c + 1)))
            stt(xn[s](), xs[sx](), rsb((c, c + 1)), g1bc(), ALU.mult, ALU.mult)

        def p0_pe(blk):
            s = blk % 2
            g = next_pg()
            for k in range(16):
                tr(PGT[g](k), xn[s]((k * 128, (k + 1) * 128)), ident)
            copy(uT(None, (blk * 128, (blk + 1) * 128)), PGT[g](), "act" if blk % 2 == 0 else "dve")

        for it in range(NKB + 2):
            if it < NKB:
                if not (ti > 0 and it < NXS):
                    p0_load(it)
                p0_act(it)
                if it == 7:
                    prefetch_win(0, reads=[xs[it % NXS]()])
                if it == 10:
                    prefetch_win(1, reads=[xs[it % NXS]()])
            if 1 <= it <= NKB:
                p0_dve(it - 1)
            if it >= 2:
                p0_pe(it - 2)
        dma("sp", rope(), rope_d[ti].rearrange("p (a b) -> p a b", a=2), "tab")
        if dbg and ti == 0:
            dma_out("sp", dbg_d["uT"][:, :], uT(), "dbg")

        def u_rhs(k, a, b):
            return uT(k, (R0 + a, R0 + b))

        def win_chunk(ci):
            tno = ci // 2
            if tno not in wtiles:
                prefetch_win(tno)
            w = wtiles[tno]
            half = ci % 2
            return lambda k, w=w, half=half: w(k, (half * 128, half * 128 + 128))

        def rope_apply(src_ps, n, col0, dst_fn, t1, t2):
            cosv = lambda p: rope(0, (col0, col0 + n), p=p)
            sinv = lambda p: rope(1, (col0, col0 + n), p=p)
            tt(t1((0, n)), src_ps((0, 128)), cosv((0, 128)), ALU.mult)
            for q in range(4):
                sp_ = (q * 32, q * 32 + 32)
                dq = q ^ 1
                dp_ = (dq * 32, dq * 32 + 32)
                tt(t2((0, n), p=dp_), src_ps(sp_), sinv(sp_), ALU.mult)
            if dst_fn is not None:
                tt(dst_fn((0, 128)), t1((0, n)), t2((0, n)), ALU.add)

        ci = 0
        for qc in range(8):
            g = next_pg()
            proj(win_chunk(ci), 16, u_rhs, RN, g); ci += 1
            s = qc % 2
            rope_apply(lambda p, g=g: PG[g]((0, RN), p=p), RN, R0, lambda p, qc=qc: qT(qc, None, p=p), tmpA[2 * s], tmpA[2 * s + 1])
        for kc in range(2):
            g = next_pg()
            proj(win_chunk(ci), 16, lambda k, a, b: uT(k, (a, b)), KVN, g); ci += 1
            s = kc % 2
            kdst = {}
            rope_apply(lambda p, g=g: PG[g]((0, KVN), p=p), KVN, 0, None, tmpK[2 * s], tmpK[2 * s + 1])
            for half in range(2):
                sp_ = (half * 64, half * 64 + 64)
                for dp_ in ((0, 64), (64, 128)):
                    tt(kT(2 * kc + half, None, p=dp_), tmpK[2 * s]((0, KVN), p=sp_), tmpK[2 * s + 1]((0, KVN), p=sp_), ALU.add)
        if dbg and ti == 0:
            dma_out("sp", dbg_d["qT"][:, :], qT(), "dbg")
            dma_out("sp", dbg_d["kT"][:, :], kT(), "dbg")

        memset(vA(None, None, (64, 128)), 1.0)
        assert ci == 10
        prefetch_win(5)
        wv_ = wtiles[5]
        ci += 2
        for blk in range(NKB):
            b_ = next_pb()
            for k in range(16):
                mm(PB[b_]((0, 256)), uT(k, (blk * 128, (blk + 1) * 128)), wv_(k), k == 0, k == 15)
            pv = Buf(ps, "ps", PB[b_].off, F32, (4, 64))
            copy(vA(blk, None, (0, 64)), pv(), "act")
        if dbg and ti == 0:
            dma_out("sp", dbg_d["v"][:, :], vA(), "dbg")

        cC_sb, ccv, ccc = tmpA[0], tmpA[1], tmpA[2]
        den = tmpA[3]
        o_sb = tmpA[4]
        pt_base = tmpA[5].off
        ptl = [SB(pt_base + i * 768, BF16, (384,)) for i in range(3)]

        def conv_gen():
            cci = 12
            for c in range(8):
                w1 = par((PC_CONVA + 3 * c + 1, PC_CONVA + 3 * c + 2))
                w0 = par((PC_CONVA + 3 * c + 0, PC_CONVA + 3 * c + 1))
                w2 = par((PC_CONVA + 3 * c + 2, PC_CONVA + 3 * c + 3))
                for which in range(3):
                    wv = win_chunk(cci)
                    cci += 1
                    pcs = pieces(RN)
                    for k in range(16):
                        for (a, b) in pcs:
                            mm(PG[0]((a, b)), wv(k), u_rhs(k, a, b), k == 0, k == 15)
                        yield "k"
                    if which == 0:
                        copy(cC_sb(), PG[0]((0, RN)), "act")
                    elif which == 1:
                        tt(ccv(), PG[0]((0, RN)), cC_sb(), ALU.mult)
                        cpending.append(lambda w1=w1: act(ccc((1, RN - 1)), ccv((1, RN - 1)), AF.Identity, scale=w1))
                        cpending.append(lambda w0=w0: stt(ccc((1, RN - 1)), ccv((0, RN - 2)), w0, ccc((1, RN - 1)), ALU.mult, ALU.add))
                        cpending.append(lambda w2=w2: stt(ccc((1, RN - 1)), ccv((2, RN)), w2, ccc((1, RN - 1)), ALU.mult, ALU.add))
                    else:
                        while cpending:
                            cpending.pop(0)()
                        tt(ya(c, (1, RN - 1)), PG[0]((1, RN - 1)), ccc((1, RN - 1)), ALU.mult)
                    yield "end"

        MQ0, MQ1 = R0 + 1, R0 + RN - 1

        def attn_gen():
            for h in range(NH):
                kvh = h // 4
                qc = h // 2
                po = (h % 2) * 64
                pr = (po, po + 64)
                Ops_ = PG[1]
                steps = []
                for kb in range(NKB):
                    a = max((kb - 1) * 128, MQ0)
                    b = min((kb + 2) * 128, MQ1)
                    if b > a:
                        steps.append((kb, a, b))
                sb_of = {}
                touched = set()

                def qk(step, si):
                    kb, a, b = step
                    bnk = next_pb()
                    sb_of[si] = bnk
                    mm(PB[bnk]((0, b - a)), kT(kvh, (kb * 128, (kb + 1) * 128), p=pr), qT(qc, (a - R0, b - R0), p=pr), True, False, skip=True)
                    for nb in (kb - 1, kb + 1):
                        a2 = max(nb * 128, a)
                        b2 = min((nb + 1) * 128, b)
                        if b2 <= a2:
                            continue
                        i0, i1 = a2 - nb * 128, b2 - nb * 128
                        mb = mask_le(i0, i1) if nb == kb - 1 else mask_ge(i0, i1)
                        mm(PB[bnk]((a2 - a, b2 - a)), ident, mb, False, False, skip=True)

                def expo(step, si):
                    kb, a, b = step
                    n = b - a
                    act(ptl[si % 3]((0, n)), PB[sb_of[si]]((0, n)), AF.Exp, bias=kbias((kb, kb + 1)), scale=0.125)

                def pv(step, si):
                    kb, a, b = step
                    pt = ptl[si % 3]
                    for nb in (kb - 1, kb, kb + 1):
                        a2 = max(nb * 128, a)
                        b2 = min((nb + 1) * 128, b)
                        if b2 <= a2:
                            continue
                        bank = a2 // 512
                        first = bank not in touched
                        touched.add(bank)
                        mm(Ops_((a2, b2)), vA(kb, kvh, None), pt((a2 - a, b2 - a)), first, False, skip=True)

                qk(steps[0], 0)
                expo(steps[0], 0)
                for si, st in enumerate(steps):
                    if si + 1 < len(steps):
                        qk(steps[si + 1], si + 1)
                        expo(steps[si + 1], si + 1)
                    yield "A"
                    pv(st, si)
                    yield "B"
                nq = MQ1 - MQ0
                while pending:
                    pending.pop(0)()
                copy(o_sb((0, nq), p=(0, 64)), Ops_((MQ0, MQ1), p=(0, 64)), "act")
                ts(den((0, nq), p=(0, 64)), Ops_((MQ0, MQ1), p=(64, 128)), esink((h, h + 1), p=(64, 128)), ALU.add)
                NPC = 4
                step_ = (nq + NPC - 1) // NPC
                for i in range(NPC):
                    c0, c1 = i * step_, min(nq, (i + 1) * step_)
                    pending.append(lambda c0=c0, c1=c1: recip(den((c0, c1), p=(0, 64)), den((c0, c1), p=(0, 64))))
                pending.append(lambda qc=qc, pr=pr, nq=nq: tt(att(qc, (1, RN - 1), p=pr), o_sb((0, nq), p=(0, 64)), den((0, nq), p=(0, 64)), ALU.mult))
                yield "E"
            while pending:
                pending.pop(0)()
                yield "A"

        pending = []
        cpending = []
        ag = attn_gen()
        a_alive = [True]
        last_tag = [None]

        def adv_ag():
            if a_alive[0]:
                try:
                    last_tag[0] = next(ag)
                except StopIteration:
                    a_alive[0] = False
                    last_tag[0] = None
            return a_alive[0]

        kcount = 0
        debt = 0
        BURST = 10
        for tag in conv_gen():
            if tag == "k":
                kcount += 1
                if kcount % 2 == 0:
                    adv_ag()
                else:
                    if cpending:
                        cpending.pop(0)()
                    elif pending:
                        pending.pop(0)()
                    if debt > 0:
                        adv_ag()
                        debt -= 1
            else:
                for i in range(BURST):
                    adv_ag()
                    if last_tag[0] == "E":
                        debt += BURST - 1 - i
                        break
        while adv_ag():
            pass
        if dbg and ti == 0:
            dma_out("sp", dbg_d["ya"][:, :], ya(), "dbg")
            dma_out("sp", dbg_d["att"][:, :], att(), "dbg")

        for f in range(16):
            s = f % 2
            gs_a, gs_b, mtmp = tmpA[3 * s], tmpA[3 * s + 1], tmpA[3 * s + 2]
            slot = wload("M", win_d[18 + f])
            wg = wview(slot, (16, 256))
            if f == 4:
                dma("pool", wm0a(), wmix_d[0][:, 0:4096], "wm0a")
                dma("pool", wm0b(), wmix_d[0][:, 4096:8192], "wm0b")
            if f % 2 == 0:
                slot2 = wload("M", woo_d[f // 2])
                wo = wview(slot2, (2, 16, 128))
            fl = f % 2
            NM = RN - 2
            um = lambda k, a, b: uT(k, (R0 + 1 + a, R0 + 1 + b))
            g = next_pg()
            proj(lambda k: wg(k, (0, 128)), 16, um, NM, g)
            act(gs_a((0, NM)), PG[g]((0, NM)), AF.Sigmoid, bias=par((PC_BGATE + f, PC_BGATE + f + 1)))
            g = next_pg()
            proj(lambda k: wo(fl, k, None), 8, lambda k, a, b: ya(k, (1 + a, 1 + b)), NM, g)
            tt(mtmp((0, NM)), PG[g]((0, NM)), gs_a((0, NM)), ALU.mult)
            g = next_pg()
            proj(lambda k: wg(k, (128, 256)), 16, um, NM, g)
            act(gs_b((0, NM)), PG[g]((0, NM)), AF.Sigmoid, bias=par((PC_BGATE + 16 + f, PC_BGATE + 16 + f + 1)))
            g = next_pg()
            proj(lambda k: wo(fl, 8 + k, None), 8, lambda k, a, b: att(k, (1 + a, 1 + b)), NM, g)
            tt(gs_b((0, NM)), PG[g]((0, NM)), gs_b((0, NM)), ALU.mult)
            tt(mT(f, (1, RN - 1)), mtmp((0, NM)), gs_b((0, NM)), ALU.add)
        if dbg and ti == 0:
            dma_out("sp", dbg_d["mT"][:, :], mT(), "dbg")

        for ob in range(8):
            r = row0 + HALO + ob * 128
            dma("sp", hbuf(ob, None), x_d[r:r + 128, :], "hx%d" % ob)
        dma("sp", hx(p=(0, 1)), x_d[row0 + HALO - 1: row0 + HALO, :], "hxe0")
        dma("sp", hx(p=(1, 2)), x_d[row0 + HALO + T: row0 + HALO + T + 1, :], "hxe1")
        dma("sp", g2bc(), gvec_d[1, :].partition_broadcast(128), "g2")

        def p7_src(blk):
            if blk < 8:
                return (0, 128), hbuf(blk, None)
            return (0, 2), hx(p=(0, 2))

        def p7_act(blk):
            pp, src = p7_src(blk)
            c = blk
            act(junk(p=pp), src, AF.Square, accum=ssb((c, c + 1), p=pp))
            act(rsb((c, c + 1), p=pp), ssb((c, c + 1), p=pp), AF.Sqrt, bias=EPS, scale=1.0 / D)

        def p7_dve(blk):
            pp, src = p7_src(blk)
            c = blk
            s3 = blk % 3
            recip(rsb((c, c + 1), p=pp), rsb((c, c + 1), p=pp))
            if blk == 8:
                tt(rsb((c, c + 1), p=pp), rsb((c, c + 1), p=pp), evalid(p=pp), ALU.mult)
            stt(u2n[s3](p=pp), src, rsb((c, c + 1), p=pp), g2bc(p=pp), ALU.mult, ALU.mult)

        def p7_pe(blk):
            s3 = blk % 3
            g = next_pg()
            if blk < 8:
                for k in range(16):
                    tr(PGT[g](k), u2n[s3]((k * 128, (k + 1) * 128)), ident)
                copy(u2T(None, (2 + blk * 128, 2 + (blk + 1) * 128)), PGT[g](), "act" if blk % 2 == 0 else "dve")
            else:
                pp = (0, 2)
                for k in range(16):
                    tr(PGT[g](k, (0, 2)), u2n[s3]((k * 128, (k + 1) * 128), p=pp), cm((0, 2), p=pp))
                copy(u2T(None, (1, RN - 1, RN - 3)), PGT[g](None, (0, 2)), "dve")

        for fs in range(4):
            if fs == 0:
                wmk = lambda k: (wm0a(k, None) if k < 8 else wm0b(k - 8, None))
            else:
                slot = wload("F", wmix_d[fs])
                wm = wview(slot, (16, 512))
                wmk = lambda k, wm=wm: wm(k, None)
            for tb in range(9):
                b_ = next_pb3()
                for k in range(16):
                    if tb < 8:
                        lhs = mT(k, (2 + tb * 128, 2 + (tb + 1) * 128))
                        o = PB3[b_]((0, 512))
                    else:
                        lhs = mT(k, (1, RN - 1, RN - 3))
                        o = PB3[b_]((0, 512), p=(0, 2))
                    mm(o, lhs, wmk(k), k == 0, k == 15)
                if tb < 8:
                    tt(hbuf(tb, (fs * 512, (fs + 1) * 512)), PB3[b_]((0, 512)), hbuf(tb, (fs * 512, (fs + 1) * 512)), ALU.add)
                else:
                    tt(hx((fs * 512, (fs + 1) * 512), p=(0, 2)), PB3[b_]((0, 512), p=(0, 2)), hx((fs * 512, (fs + 1) * 512), p=(0, 2)), ALU.add)
                if fs == 3 and PIPE7:
                    p7_act(tb)
                    if tb >= 1:
                        p7_dve(tb - 1)
                    if tb >= 3:
                        p7_pe(tb - 3)
        if PIPE7:
            p7_dve(8)
            for blk in (6, 7, 8):
                p7_pe(blk)
        else:
            for blk in range(9):
                p7_act(blk)
                p7_dve(blk)
                p7_pe(blk)
        if dbg and ti == 0:
            dma_out("sp", dbg_d["h1"][:, :], hbuf(), "dbg")
            dma_out("sp", dbg_d["u2T"][:, :], u2T(), "dbg")

        def ffn_up(gi):
            slot_a = wload("F", wup_d[2 * gi])
            wa = wview(slot_a, (16, 512))
            slot_g = wload("F", wup_d[2 * gi + 1])
            wgv = wview(slot_g, (16, 512))
            for j in range(GRP):
                for which in range(2):
                    w = wa if which == 0 else wgv
                    chunk = (gi * GRP + j) + (0 if which == 0 else NFF)
                    pb_ = psb[which]
                    if FFN_XB:
                        g = ug_i[0] % 2
                        ug_i[0] += 1
                        xslot = xb_i[0] % 4
                        xb_i[0] += 1
                        xv = XB((2 * xslot, 2 * xslot + 2))
                        for k in range(16):
                            wk = w(k, (j * 128, (j + 1) * 128))
                            mm(UG[g]((0, 512)), wk, u2T(k, (2, 514)), k == 0, k == 15)
                            mm(UG[g]((512, 1024)), wk, u2T(k, (514, 1026)), k == 0, k == 15)
                            mm(xv, wk, u2T(k, (1, RN - 1, RN - 3)), k == 0, k == 15, skip=True)
                        copy(pb_((1, T + 1)), UG[g]((0, T)), "act")
                        copy(pb_((0, FN, FN - 1)), xv, "act")
                    else:
                        g = next_pg()
                        proj(lambda k, w=w, j=j: w(k, (j * 128, (j + 1) * 128)), 16, lambda k, a, b: u2T(k, (1 + a, 1 + b)), FN, g)
                        copy(pb_((0, FN)), PG[g]((0, FN)), "act")
                    dst = (a_c if which == 0 else gv_c)[j % 2]
                    w0 = par((PC_FCW + 3 * chunk, PC_FCW + 3 * chunk + 1))
                    w1 = par((PC_FCW + 3 * chunk + 1, PC_FCW + 3 * chunk + 2))
                    w2 = par((PC_FCW + 3 * chunk + 2, PC_FCW + 3 * chunk + 3))
                    bb = par((PC_FCB + chunk, PC_FCB + chunk + 1))
                    act(dst(), pb_((1, T + 1)), AF.Identity, bias=bb, scale=w1)
                    stt(dst(), pb_((0, T)), w0, dst(), ALU.mult, ALU.add)
                    stt(dst(), pb_((2, T + 2)), w2, dst(), ALU.mult, ALU.add)
                act(sil[0](), a_c[j % 2](), AF.Silu)
                tt(actT[gi % 2](j, None), sil[0](), gv_c[j % 2](), ALU.mult)

        def p9_act(ob):
            c = ob
            act(junk(), hbuf(ob, None), AF.Square, accum=ssb((c, c + 1)))
            act(rsb((c, c + 1)), ssb((c, c + 1)), AF.Sqrt, bias=EPS, scale=1.0 / D)

        def p9_dve(ob):
            c = ob
            s2 = ob % 2
            recip(rsb((c, c + 1)), rsb((c, c + 1)))
            stt(ost[s2](), hbuf(ob, None), rsb((c, c + 1)), gFbc(), ALU.mult, ALU.mult)
            r = ti * T + ob * 128
            dma_out("sp", out_d[r:r + 128, :], ost[s2](), "ost%d" % s2)

        def ffn_down(gi, last):
            slot = wload("F", wdn_d[gi])
            wd = wview(slot, (GRP, D))
            for tb in range(8):
                for fs in range(4):
                    if FFN_XB:
                        bank = PB3[next_pb3()]
                    else:
                        bank = DB4[db4_i[0] % 4]
                        db4_i[0] += 1
                    for kk in range(GRP):
                        mm(bank((0, 512)), actT[gi % 2](kk, (tb * 128, (tb + 1) * 128)), wd(kk, (fs * 512, (fs + 1) * 512)), kk == 0, kk == GRP - 1)
                    tt(hbuf(tb, (fs * 512, (fs + 1) * 512)), bank((0, 512)), hbuf(tb, (fs * 512, (fs + 1) * 512)), ALU.add)
                if last and PIPE9:
                    p9_act(tb)
                    if tb >= 1:
                        p9_dve(tb - 1)
            if last and PIPE9:
                p9_dve(7)
            if last and not PIPE9:
                for ob in range(8):
                    p9_act(ob)
                    p9_dve(ob)

        ffn_up(0)
        for gi in range(NGRP):
            if gi + 1 < NGRP:
                ffn_up(gi + 1)
            if gi == NGRP - 1:
                dma("sp", gFbc(), gvec_d[2, :].partition_broadcast(128), "gF")
            ffn_down(gi, gi == NGRP - 1)
            if gi == NGRP - 2 and ti + 1 < NT:
                dma("pool", g1bc(), gvec_d[0, :].partition_broadcast(128), "g1p")
                for b0 in range(NXS):
                    p0_load(b0, q="pool", r0=(ti + 1) * T)

    P.emit(nc, es)
    es.close()
    return nc


def _prep_shared(inp):
    f32 = np.float32
    w_in = np.asarray(inp["w_in"], dtype=f32)[0]
    chunk_cols = []
    for qc in range(8):
        chunk_cols.append(np.arange(3072 + 128 * qc, 3072 + 128 * qc + 128))
    for kc in range(2):
        chunk_cols.append(np.arange(4096 + 128 * kc, 4096 + 128 * kc + 128))
    chunk_cols.append(np.arange(4352, 4352 + 128))
    chunk_cols.append(np.arange(4352 + 128, 4352 + 256))
    for c in range(8):
        for c0 in (1024 + 128 * c, 2048 + 128 * c, 128 * c):
            chunk_cols.append(np.arange(c0, c0 + 128))
    for f in range(16):
        chunk_cols.append(np.arange(4608 + 128 * f, 4608 + 128 * f + 128))
        chunk_cols.append(np.arange(6656 + 128 * f, 6656 + 128 * f + 128))
    assert len(chunk_cols) == 68
    allc = np.concatenate(chunk_cols)
    wsel = w_in[:, allc]
    w_in_t = np.ascontiguousarray(wsel.reshape(16, 128, 34, 256).transpose(2, 1, 0, 3)).reshape(34, 128, 4096)

    w_oa = np.asarray(inp["w_out_a"], dtype=f32)[0]
    w_ob = np.asarray(inp["w_o_attn"], dtype=f32)[0]
    wcat = np.concatenate([w_oa, w_ob], axis=0)
    w_oo_t = np.ascontiguousarray(wcat.reshape(16, 128, 8, 2, 128).transpose(2, 1, 3, 0, 4)).reshape(8, 128, 4096)

    w_mix = np.asarray(inp["w_mix_out"], dtype=f32)[0]
    w_mix_t = np.ascontiguousarray(w_mix.reshape(16, 128, 4, 512).transpose(2, 1, 0, 3)).reshape(4, 128, 8192)

    w_up = np.asarray(inp["ffn_w_up"], dtype=f32)[0]
    wu = w_up.reshape(16, 128, 2, NGRP, 512)
    w_up_t = np.ascontiguousarray(wu.transpose(3, 2, 1, 0, 4)).reshape(2 * NGRP, 128, 8192)

    w_dn = np.asarray(inp["ffn_w_down"], dtype=f32)[0]
    wd = w_dn.reshape(NGRP, GRP, 128, D)
    w_dn_t = np.ascontiguousarray(wd.transpose(0, 2, 1, 3)).reshape(NGRP, 128, 8192)

    params = np.zeros((128, NPAR), dtype=f32)
    ca = np.asarray(inp["conv_a_w"], dtype=f32)[0]
    params[:, PC_CONVA:PC_CONVA + 24] = ca.reshape(3, 8, 128).transpose(2, 1, 0).reshape(128, 24)
    bg = np.asarray(inp["b_gate"], dtype=f32)[0]
    params[:, PC_BGATE:PC_BGATE + 32] = bg.reshape(32, 128).T
    fw = np.asarray(inp["ffn_conv_w"], dtype=f32)[0]
    params[:, PC_FCW:PC_FCW + 264] = fw.reshape(3, 88, 128).transpose(2, 1, 0).reshape(128, 264)
    fb = np.asarray(inp["ffn_conv_b"], dtype=f32)[0]
    params[:, PC_FCB:PC_FCB + 88] = fb.reshape(88, 128).T
    sk = np.asarray(inp["sink_logits"], dtype=f32)[0]
    params[:, PC_SINK:PC_SINK + 16] = np.broadcast_to(sk[None, :], (128, 16))

    gvec = np.stack([np.asarray(inp["norm_mix_g"], dtype=f32)[0],
                     np.asarray(inp["norm_ffn_g"], dtype=f32)[0],
                     np.asarray(inp["norm_final_g"], dtype=f32)], axis=0)

    j = np.arange(128)
    cmask = np.zeros((128, 512), dtype=f32)
    cmask[:, 0:128] = np.eye(128, dtype=f32)
    if MASK_PE:
        cmask[:, 128:256] = np.where(j[:, None] <= j[None, :], 0.0, -30000.0).astype(f32)
        cmask[:, 384:512] = np.where(j[:, None] >= j[None, :], 0.0, -30000.0).astype(f32)
    else:
        cmask[:, 128:256] = (j[:, None] <= j[None, :]).astype(f32)
        cmask[:, 384:512] = (j[:, None] >= j[None, :]).astype(f32)
    return dict(w_in_t=w_in_t, w_oo_t=w_oo_t, w_mix_t=w_mix_t, w_up_t=w_up_t, w_dn_t=w_dn_t,
                params=params, gvec=gvec, cmask=cmask)


def _prep_core(x2d_pad, c):
    f32 = np.float32
    base = c * TOKC
    xs = np.ascontiguousarray(x2d_pad[base: base + XROWS])
    half = HD // 2
    inv_freq = 10000.0 ** (-(np.arange(0, half, dtype=np.float64) / float(half)))
    p = np.arange(128)
    fidx = p % 32
    sign = np.where((p % 64) < 32, 1.0, -1.0).astype(f32)
    rope = np.zeros((NT, 128, 2 * KVN), dtype=f32)
    kbias = np.zeros((NT, 128, NKB), dtype=f32)
    evalid = np.zeros((NT, 2, 1), dtype=f32)
    for ti in range(NT):
        s = base + ti * T
        pos = (s - HALO + np.arange(KVN)).astype(np.int64)
        posf = np.clip(pos, 0, SEQ - 1).astype(np.float64)
        ang = posf[None, :] * inv_freq[fidx][:, None]
        rope[ti, :, 0:KVN] = np.cos(ang).astype(f32)
        rope[ti, :, KVN:] = (np.sin(ang).astype(f32) * sign[:, None]).astype(f32)
        kpos = (s - HALO) + np.arange(NKB)[None, :] * 128 + p[:, None]
        kbias[ti] = np.where((kpos >= 0) & (kpos < SEQ), 0.0, -30000.0).astype(f32)
        evalid[ti, 0, 0] = 1.0 if (s - 1) >= 0 else 0.0
        evalid[ti, 1, 0] = 1.0 if (s + T) < SEQ else 0.0
    return dict(x=xs, rope=rope, kbias=kbias, evalid=evalid)


_NC_CACHE = {}


def kernel(**inputs):
    x = np.asarray(inputs["x"], dtype=np.float32)
    x2d = x.reshape(SEQ, D)
    xpad = np.zeros((SEQ + 2 * HALO, D), dtype=np.float32)
    xpad[HALO:HALO + SEQ] = x2d
    shared = _prep_shared(inputs)
    in_maps = []
    for c in range(NCORE):
        m = dict(shared)
        m.update(_prep_core(xpad, c))
        in_maps.append(m)
    if "nc" not in _NC_CACHE:
        _NC_CACHE["nc"] = build_program(DEBUG)
    nc = _NC_CACHE["nc"]
    res = run_bass_kernel_spmd(nc, in_maps, core_ids=list(range(NCORE)))
    if DEBUG:
        _NC_CACHE["last"] = res
    out = np.concatenate([np.asarray(r["out"], dtype=np.float32) for r in res.results], axis=0)
    return out.reshape(1, SEQ, D)
```

```python
import numpy as np
from contextlib import ExitStack

import concourse.bass as bass
import concourse.mybir as mybir
from concourse.bass_utils import run_bass_kernel_spmd

F32 = mybir.dt.float32
BF16 = mybir.dt.bfloat16
AF = mybir.ActivationFunctionType
ALU = mybir.AluOpType

D = 2048
SEQ = 16384
NCORE = 8
TOKC = SEQ // NCORE
T = 1024
NT = TOKC // T
HALO = 256
XROWS = TOKC + 2 * HALO
KVN = T + 2 * HALO
R0 = 254
RN = 1028
NKB = KVN // 128
D_CONV = 1024
NH = 16
NKV = 4
HD = 64
D_FF = 5632
NFF = D_FF // 128
EPS = 1e-6
FN = T + 2
GRP = 4
NGRP = NFF // GRP

DEBUG = False
MASK_PE = True
PIPE7 = True
PIPE9 = True
FFN_XB = False

PC_CONVA = 0
PC_BGATE = PC_CONVA + 24
PC_FCW = PC_BGATE + 32
PC_FCB = PC_FCW + 264
PC_SINK = PC_FCB + 88
NPAR = PC_SINK + 16


class View:
    __slots__ = ("ap", "space", "p0", "p1", "b0", "b1", "ivs")

    def __init__(self, ap, space, p0, p1, b0, b1, ivs=None):
        self.ap = ap
        self.space = space
        self.p0, self.p1, self.b0, self.b1 = p0, p1, b0, b1
        self.ivs = ivs


def _ivs_overlap(a, b):
    for (x0, x1) in a:
        for (y0, y1) in b:
            if x0 < y1 and y0 < x1:
                return True
    return False


class Buf:
    def __init__(self, arena, space, off, dtype, shape):
        self.arena = arena
        self.space = space
        self.off = off
        self.dtype = dtype
        self.esz = 2 if dtype == BF16 else 4
        self.shape = tuple(shape)
        n = 1
        for s in shape:
            n *= s
        self.nelem = n
        self.nbytes = n * self.esz
        assert off % 4 == 0 and self.nbytes % 4 == 0, (off, shape)
        st = []
        acc = 1
        for s in reversed(shape):
            st.append(acc)
            acc *= s
        self.strides = tuple(reversed(st))

    def end(self):
        return self.off + self.nbytes

    def __call__(self, *idx, p=(0, 128)):
        p0, p1 = p
        base = self.arena[p0:p1, self.off // 4:(self.off + self.nbytes) // 4]
        if self.dtype == BF16:
            base = base.bitcast(BF16)
        if len(self.shape) == 2:
            base = base.rearrange("p (a b) -> p a b", a=self.shape[0])
        elif len(self.shape) == 3:
            base = base.rearrange("p (a b c) -> p a b c", a=self.shape[0], b=self.shape[1])
        elif len(self.shape) == 4:
            base = base.rearrange("p (a b c d) -> p a b c d", a=self.shape[0], b=self.shape[1], c=self.shape[2])
        idx = list(idx) + [None] * (len(self.shape) - len(idx))
        key = [slice(None)]
        lo = 0
        hi = 0
        for d, ix in enumerate(idx):
            n = self.shape[d]
            if ix is None:
                key.append(slice(None))
                hi += (n - 1) * self.strides[d]
            elif isinstance(ix, int):
                assert 0 <= ix < n, (ix, n, self.shape)
                key.append(ix)
                lo += ix * self.strides[d]
                hi += ix * self.strides[d]
            else:
                a, b = ix[0], ix[1]
                step = ix[2] if len(ix) > 2 else 1
                assert 0 <= a < b <= n, (ix, n, self.shape)
                key.append(slice(a, b, step) if step != 1 else slice(a, b))
                last = a + ((b - 1 - a) // step) * step
                lo += a * self.strides[d]
                hi += last * self.strides[d]
        ap = base[tuple(key)]
        ivs = None
        if len(self.shape) >= 2:
            rngs = []
            for d, ix in enumerate(idx):
                n = self.shape[d]
                if ix is None:
                    rngs.append((0, n, 1))
                elif isinstance(ix, int):
                    rngs.append((ix, ix + 1, 1))
                else:
                    rngs.append((ix[0], ix[1], ix[2] if len(ix) > 2 else 1))
            a_, b_, st_ = rngs[-1]
            if st_ == 1:
                inner = [(a_, b_)]
            else:
                inner = [(j, j + 1) for j in range(a_, b_, st_)]
            outer = [0]
            for d in range(len(self.shape) - 1):
                a2, b2, st2 = rngs[d]
                outer = [o + j * self.strides[d] for o in outer for j in range(a2, b2, st2)]
            if len(outer) * len(inner) <= 64:
                raw = sorted((o + i0, o + i1) for o in outer for (i0, i1) in inner)
                merged = []
                for (x0, x1) in raw:
                    if merged and x0 <= merged[-1][1]:
                        merged[-1][1] = max(merged[-1][1], x1)
                    else:
                        merged.append([x0, x1])
                if len(merged) > 1:
                    ivs = [(self.off + m0 * self.esz, self.off + m1 * self.esz) for (m0, m1) in merged]
        return View(ap, self.space, p0, p1, self.off + lo * self.esz, self.off + (hi + 1) * self.esz, ivs)


class Op:
    __slots__ = ("eng", "fn", "reads", "writes", "dma_sem", "dma_cnt", "deps", "idx", "ms", "group")

    def __init__(self, eng, fn, reads, writes, dma_sem=None, group=False):
        self.eng = eng
        self.fn = fn
        self.reads = reads
        self.writes = writes
        self.dma_sem = dma_sem
        self.dma_cnt = 0
        self.deps = []
        self.idx = -1
        self.ms = 0
        self.group = group


class Prog:
    ENGS = ("pe", "act", "dve", "pool", "sp")

    def __init__(self):
        self.ops = []
        self.recs = {"sb": [], "ps": []}
        self.dma_counts = {}
        self.final_waits = []

    def add(self, eng, fn, reads=(), writes=(), dma_sem=None, group=False):
        op = Op(eng, fn, [v for v in reads if v is not None], [v for v in writes if v is not None], dma_sem, group)
        op.idx = len(self.ops)
        if dma_sem is not None:
            self.dma_counts[dma_sem] = self.dma_counts.get(dma_sem, 0) + 16
            op.dma_cnt = self.dma_counts[dma_sem]
        deps = set()
        ops = self.ops
        pe_self = (eng == "pe")
        is_dma = dma_sem is not None
        for v in op.reads:
            for r in self.recs[v.space]:
                if r[4] == "w" and r[0] < v.p1 and v.p0 < r[1] and r[2] < v.b1 and v.b0 < r[3]:
                    if (v.ivs is not None or r[6] is not None) and not _ivs_overlap(v.ivs or [(v.b0, v.b1)], r[6] or [(r[2], r[3])]):
                        continue
                    deps.add(r[5])
        for v in op.writes:
            for r in self.recs[v.space]:
                if r[0] < v.p1 and v.p0 < r[1] and r[2] < v.b1 and v.b0 < r[3]:
                    if (v.ivs is not None or r[6] is not None) and not _ivs_overlap(v.ivs or [(v.b0, v.b1)], r[6] or [(r[2], r[3])]):
                        continue
                    deps.add(r[5])
        if pe_self:
            deps = {d for d in deps if not (ops[d].eng == "pe" and ops[d].dma_sem is None)}
        op.deps = sorted(deps)
        for v in op.writes:
            lst = self.recs[v.space]
            if v.ivs is None:
                lst[:] = [r for r in lst if not (v.p0 <= r[0] and r[1] <= v.p1 and v.b0 <= r[2] and r[3] <= v.b1)]
            else:
                lst[:] = [r for r in lst if not (v.p0 <= r[0] and r[1] <= v.p1 and r[6] == v.ivs)]
            lst.append([v.p0, v.p1, v.b0, v.b1, "w", op.idx, v.ivs])
        for v in op.reads:
            lst = self.recs[v.space]
            done = False
            if not is_dma:
                for r in lst:
                    if r[4] == "r" and r[0] == v.p0 and r[1] == v.p1 and r[2] == v.b0 and r[3] == v.b1 and r[6] == v.ivs:
                        o = ops[r[5]]
                        if o.eng == eng and o.dma_sem is None:
                            r[5] = op.idx
                            done = True
                            break
            if not done:
                lst.append([v.p0, v.p1, v.b0, v.b1, "r", op.idx, v.ivs])
        self.ops.append(op)
        return op

    def emit(self, nc, es):
        ops = self.ops
        needed = set()
        for op in ops:
            for d in op.deps:
                needed.add(d)
        cnt = {e: 0 for e in self.ENGS}
        for op in ops:
            if op.dma_sem is None and op.idx in needed:
                cnt[op.eng] += 1
                op.ms = cnt[op.eng]
        esem = {e: es.enter_context(nc.semaphore("ms_" + e)) for e in self.ENGS}
        dsem = {k: es.enter_context(nc.semaphore("dq_" + k)) for k in self.dma_counts}
        block = es.enter_context(nc.Block())
        by_eng = {e: [op for op in ops if op.eng == e] for e in self.ENGS}
        final_total = dict(self.dma_counts)
        out_sems = list(self.final_waits)

        def run(engname, eng):
            waited = {}
            for op in by_eng[engname]:
                for d in op.deps:
                    o = ops[d]
                    if o.dma_sem is not None:
                        sem = dsem[o.dma_sem]
                        val = final_total[o.dma_sem] if o.group else o.dma_cnt
                        key = "d" + o.dma_sem
                    else:
                        sem = esem[o.eng]
                        val = o.ms
                        key = "e" + o.eng
                    if waited.get(key, 0) >= val:
                        continue
                    waited[key] = val
                    eng.wait_ge(sem, val)
                ins = op.fn(eng)
                if op.dma_sem is not None:
                    ins.then_inc(dsem[op.dma_sem], 16)
                elif op.ms:
                    ins.then_inc(esem[op.eng], 1)
            if engname == "sp":
                for k in out_sems:
                    eng.wait_ge(dsem[k], final_total[k])

        block.tensor(lambda e: run("pe", e))
        block.scalar(lambda e: run("act", e))
        block.vector(lambda e: run("dve", e))
        block.gpsimd(lambda e: run("pool", e))
        block.sync(lambda e: run("sp", e))


def pieces(n, step=512):
    out = []
    a = 0
    while a < n:
        b = min(a + step, n)
        out.append((a, b))
        a = b
    if len(out) >= 2 and (out[-1][1] - out[-1][0]) <= 8:
        out = [out[-1]] + out[:-1]
    return out


def build_program(dbg=False):
    nc = bass.Bass("TRN2", target_bir_lowering=False)
    P = Prog()
    es = ExitStack()

    x_d = nc.dram_tensor("x", [XROWS, D], F32, kind="ExternalInput").ap()
    win_d = nc.dram_tensor("w_in_t", [34, 128, 4096], F32, kind="ExternalInput").ap()
    woo_d = nc.dram_tensor("w_oo_t", [8, 128, 4096], F32, kind="ExternalInput").ap()
    wmix_d = nc.dram_tensor("w_mix_t", [4, 128, 8192], F32, kind="ExternalInput").ap()
    wup_d = nc.dram_tensor("w_up_t", [2 * NGRP, 128, 8192], F32, kind="ExternalInput").ap()
    wdn_d = nc.dram_tensor("w_dn_t", [NGRP, 128, 8192], F32, kind="ExternalInput").ap()
    par_d = nc.dram_tensor("params", [128, NPAR], F32, kind="ExternalInput").ap()
    gvec_d = nc.dram_tensor("gvec", [3, D], F32, kind="ExternalInput").ap()
    rope_d = nc.dram_tensor("rope", [NT, 128, 2 * KVN], F32, kind="ExternalInput").ap()
    kbias_d = nc.dram_tensor("kbias", [NT, 128, NKB], F32, kind="ExternalInput").ap()
    eval_d = nc.dram_tensor("evalid", [NT, 2, 1], F32, kind="ExternalInput").ap()
    cmask_d = nc.dram_tensor("cmask", [128, 512], F32, kind="ExternalInput").ap()
    out_d = nc.dram_tensor("out", [TOKC, D], F32, kind="ExternalOutput").ap()
    dbg_d = {}
    if dbg:
        dbg_d["uT"] = nc.dram_tensor("dbg_uT", [128, 16 * KVN], BF16, kind="ExternalOutput").ap()
        dbg_d["ya"] = nc.dram_tensor("dbg_ya", [128, 8 * RN], BF16, kind="ExternalOutput").ap()
        dbg_d["qT"] = nc.dram_tensor("dbg_qT", [128, 8 * RN], BF16, kind="ExternalOutput").ap()
        dbg_d["kT"] = nc.dram_tensor("dbg_kT", [128, 4 * KVN], BF16, kind="ExternalOutput").ap()
        dbg_d["v"] = nc.dram_tensor("dbg_v", [128, NKB * 4 * 128], BF16, kind="ExternalOutput").ap()
        dbg_d["att"] = nc.dram_tensor("dbg_att", [128, 8 * RN], BF16, kind="ExternalOutput").ap()
        dbg_d["mT"] = nc.dram_tensor("dbg_mT", [128, 16 * RN], BF16, kind="ExternalOutput").ap()
        dbg_d["h1"] = nc.dram_tensor("dbg_h1", [128, 8 * D], F32, kind="ExternalOutput").ap()
        dbg_d["u2T"] = nc.dram_tensor("dbg_u2T", [128, 16 * RN], BF16, kind="ExternalOutput").ap()

    SB_BYTES = 206 * 1024
    sb = es.enter_context(nc.sbuf_tensor("arena", [128, SB_BYTES // 4], F32))
    ps = es.enter_context(nc.psum_tensor("psarena", [128, 4096], F32))

    cur = [0]

    def alloc(nbytes):
        o = cur[0]
        cur[0] += (nbytes + 31) // 32 * 32
        return o

    def SB(off, dtype, shape):
        return Buf(sb, "sb", off, dtype, shape)

    o_cm = alloc(512 * 2)
    cm = SB(o_cm, BF16, (512,))
    o_par = alloc(NPAR * 4)
    par = SB(o_par, F32, (NPAR,))
    o_esink = alloc(16 * 4)
    esink = SB(o_esink, F32, (16,))
    o_kb = alloc(NKB * 4)
    kbias = SB(o_kb, F32, (NKB,))
    o_ev = alloc(4)
    evalid = SB(o_ev, F32, (1,))
    o_ss = alloc(16 * 4)
    ssb = SB(o_ss, F32, (16,))
    o_rs = alloc(16 * 4)
    rsb = SB(o_rs, F32, (16,))
    o_C = alloc(16 * KVN * 2)
    uT = SB(o_C, BF16, (16, KVN))
    o_D = alloc(8 * RN * 2)
    ya = SB(o_D, BF16, (8, RN))
    assert cur[0] - o_C >= 8 * D * 4, (cur[0] - o_C)
    hbuf = SB(o_C, F32, (8, D))
    o_E = alloc(8 * RN * 2)
    att = SB(o_E, BF16, (8, RN))
    o_B = alloc(2 * KVN * 4)
    rope = SB(o_B, F32, (2, KVN))
    EB = cur[0] - o_E
    NXS = 3
    hx = SB(o_E, F32, (D,))
    u2n = [SB(o_E + 8192 + i * 4096, BF16, (D,)) for i in range(3)]
    g2bc = SB(o_E + 20480, F32, (D,))
    assert 20480 + 8192 <= EB, EB
    o_F = alloc(8 * RN * 2 + 4 * KVN * 2 + NKB * 4 * 128 * 2 + 64)
    qT = SB(o_F, BF16, (8, RN))
    kT = SB(o_F + 8 * RN * 2, BF16, (4, KVN))
    vA = SB(o_F + 8 * RN * 2 + 4 * KVN * 2, BF16, (NKB, 4, 128))
    g1bc = SB(o_F, F32, (D,))
    mT = SB(o_F, BF16, (16, RN))
    u2T = SB(o_F, BF16, (16, RN))
    assert 16 * RN * 2 + 8192 <= cur[0] - o_F
    xs = [SB(o_F + 8192 + i * 8192, F32, (D,)) for i in range(NXS)]
    xn = [SB(o_F + 8192 + NXS * 8192 + i * 4096, BF16, (D,)) for i in range(2)]
    assert 8192 + NXS * 8192 + 8192 <= cur[0] - o_F
    junk0 = SB(o_E + 24576, BF16, (D,))
    assert 24576 + 4096 <= EB
    wm0b = SB(o_F + 16 * RN * 2, BF16, (8, 512))
    o_G = alloc(25 * 1024)
    o_H = alloc(32 * 1024)
    o_I = alloc(8 * 1024)
    wm0a = SB(o_I, BF16, (8, 512))
    assert cur[0] <= SB_BYTES, cur[0]
    GHI = cur[0] - o_G
    TM = 4128
    tmpA = [SB(o_G + i * TM, F32, (RN,)) for i in range(6)]
    tmpK = [SB(o_G + i * 6144, F32, (KVN,)) for i in range(4)]
    den = SB(o_G, F32, (RN,))
    rden = SB(o_G + TM, F32, (RN,))
    ptile = [SB(o_G + 2 * TM + i * 768, BF16, (384,)) for i in range(3)]
    assert 2 * TM + 3 * 768 <= 25 * 1024
    ringM = [SB(o_H + i * 8192, BF16, (4096,)) for i in range(4)]
    ringF = [SB(o_G + i * 16384, BF16, (8192,)) for i in range(3)]
    o_act = o_G + 3 * 16384
    actT = [SB(o_act + i * 8192, BF16, (GRP, T)) for i in range(2)]
    assert o_act + 16384 <= cur[0]
    junk = SB(o_act + 8192, BF16, (D,))
    FT = 4128
    psb = [SB(o_E + i * FT, F32, (FN,)) for i in range(2)]
    a_c = [SB(o_E + 2 * FT + i * 4096, F32, (T,)) for i in range(2)]
    gv_c = [SB(o_E + 2 * FT + 8192 + i * 4096, F32, (T,)) for i in range(2)]
    sil = [SB(o_E + 2 * FT + 16384 + i * 4096, F32, (T,)) for i in range(1)]
    assert 2 * FT + 16384 + 4096 <= EB, (EB,)
    ost = [SB(o_E + i * 8192, F32, (D,)) for i in range(2)]
    gFbc = SB(o_E + 16384, F32, (D,))

    def PS(col, dtype, shape):
        return Buf(ps, "ps", col * 4, dtype, shape)

    PG = [PS(0, F32, (1536,)), PS(1536, F32, (1536,))]
    PGT = [PS(0, BF16, (16, 128)), PS(1536, BF16, (16, 128))]
    PB = [PS(3072, F32, (512,)), PS(3584, F32, (512,))]
    PB3 = [PS(2560, F32, (512,)), PS(3072, F32, (512,)), PS(3584, F32, (512,)), PS(1024, F32, (512,))]
    UG = [PS(0, F32, (1024,)), PS(1024, F32, (1024,))]
    XB = PS(2048, F32, (512,))
    DB4 = [PS(3072, F32, (512,)), PS(3584, F32, (512,)), PS(1024, F32, (512,)), PS(2560, F32, (512,))]
    db4_i = [0]

    def dma(q, out_v, in_ap, sem, group=False, reads=()):
        P.add(q, lambda e, o=out_v.ap, i=in_ap: e.dma_start(out=o, in_=i), reads=reads, writes=[out_v], dma_sem=sem, group=group)

    def dma_out(q, out_ap, in_v, sem):
        P.add(q, lambda e, o=out_ap, i=in_v.ap: e.dma_start(out=o, in_=i), reads=[in_v], writes=[], dma_sem=sem)
        if sem not in P.final_waits:
            P.final_waits.append(sem)

    def act(out_v, in_v, func, bias=None, scale=None, accum=None, extra_reads=()):
        kw = {}
        rd = [in_v] + list(extra_reads)
        wr = [out_v]
        if bias is not None:
            if isinstance(bias, View):
                kw["bias"] = bias.ap
                rd.append(bias)
            else:
                kw["bias"] = bias
        if scale is not None:
            if isinstance(scale, View):
                kw["scale"] = scale.ap
                rd.append(scale)
            else:
                kw["scale"] = scale
        if accum is not None:
            kw["accum_out"] = accum.ap
            wr.append(accum)
        P.add("act", lambda e, o=out_v.ap, i=in_v.ap, f=func, kw=kw: e.activation(out=o, in_=i, func=f, **kw), reads=rd, writes=wr)

    def tt(out_v, in0, in1, op, eng="dve"):
        P.add(eng, lambda e, o=out_v.ap, a=in0.ap, b=in1.ap, op=op: e.tensor_tensor(out=o, in0=a, in1=b, op=op), reads=[in0, in1], writes=[out_v])

    def stt(out_v, in0, scalar, in1, op0, op1, eng="dve"):
        rd = [in0, in1]
        s = scalar
        if isinstance(scalar, View):
            rd.append(scalar)
            s = scalar.ap
        P.add(eng, lambda e, o=out_v.ap, a=in0.ap, s=s, b=in1.ap, op0=op0, op1=op1: e.scalar_tensor_tensor(out=o, in0=a, scalar=s, in1=b, op0=op0, op1=op1), reads=rd, writes=[out_v])

    def ts(out_v, in0, s1, op0, eng="dve"):
        rd = [in0]
        s = s1
        if isinstance(s1, View):
            rd.append(s1)
            s = s1.ap
        P.add(eng, lambda e, o=out_v.ap, a=in0.ap, s=s, op0=op0: e.tensor_scalar(out=o, in0=a, scalar1=s, scalar2=None, op0=op0), reads=rd, writes=[out_v])

    def recip(out_v, in_v):
        P.add("dve", lambda e, o=out_v.ap, i=in_v.ap: e.reciprocal(out=o, in_=i), reads=[in_v], writes=[out_v])

    def copy(out_v, in_v, eng):
        if eng == "act":
            P.add("act", lambda e, o=out_v.ap, i=in_v.ap: e.activation(out=o, in_=i, func=AF.Copy), reads=[in_v], writes=[out_v])
        else:
            P.add(eng, lambda e, o=out_v.ap, i=in_v.ap: e.tensor_copy(out=o, in_=i), reads=[in_v], writes=[out_v])

    def memset(out_v, val, eng="dve"):
        P.add(eng, lambda e, o=out_v.ap, v=val: e.memset(o, v), reads=[], writes=[out_v])

    def mm(out_v, lhsT, rhs, start, stop, skip=False):
        P.add("pe", lambda e, o=out_v.ap, l=lhsT.ap, r=rhs.ap, s=start, t=stop, sk=skip: e.matmul(o, lhsT=l, rhs=r, start=s, stop=t, skip_group_check=sk), reads=[lhsT, rhs], writes=[out_v])

    def tr(out_v, in_v, ident_v):
        P.add("pe", lambda e, o=out_v.ap, i=in_v.ap, d=ident_v.ap: e.transpose(o, i, d), reads=[in_v, ident_v], writes=[out_v])

    ring_state = {"M": 0, "F": 0}

    def wload(kind, dram_ap, reads=()):
        if kind == "M":
            i = ring_state["M"] % 4
            ring_state["M"] += 1
            slot = ringM[i]
            sem = "rm%d" % i
        else:
            i = ring_state["F"] % 3
            ring_state["F"] += 1
            slot = ringF[i]
            sem = "rf%d" % i
        dma("pool", slot(), dram_ap, sem, reads=reads)
        return slot

    def wview(slot, shape):
        return Buf(sb, "sb", slot.off, BF16, shape)

    dma("pool", cm(), cmask_d[:, :], "c_pool", group=True)
    dma("sp", par(), par_d[:, :], "c_sp", group=True)
    ident = cm((0, 128))
    mask_le = lambda a, b: cm((128 + a, 128 + b))
    mask_ge = lambda a, b: cm((384 + a, 384 + b))
    act(esink(), par((PC_SINK, PC_SINK + 16)), AF.Exp)

    psg = [0]

    def next_pg():
        g = psg[0] % 2
        psg[0] += 1
        return g

    psb_i = [0]

    def next_pb():
        b = psb_i[0] % 2
        psb_i[0] += 1
        return b

    pb3_i = [0]

    def next_pb3():
        b = pb3_i[0] % len(PB3)
        pb3_i[0] += 1
        return b

    ug_i = [0]
    xb_i = [0]

    def proj(wv, nk, rhs_fn, ncols, grp):
        pcs = pieces(ncols)
        for k in range(nk):
            for (a, b) in pcs:
                mm(PG[grp]((a, b)), wv(k), rhs_fn(k, a, b), k == 0, k == nk - 1)

    xsem = ["xs%d" % i for i in range(NXS)]

    for ti in range(NT):
        row0 = ti * T
        dma("sp", kbias(), kbias_d[ti], "tab2")
        dma("sp", evalid(p=(0, 2)), eval_d[ti], "tab3")
        if ti == 0:
            dma("sp", g1bc(), gvec_d[0, :].partition_broadcast(128), "g1")
        wtiles = {}

        def prefetch_win(tno, reads=()):
            slot = wload("M", win_d[tno], reads=reads)
            wtiles[tno] = wview(slot, (16, 256))

        def p0_load(blk, q="sp", r0=None):
            sx = blk % NXS
            rr = row0 if r0 is None else r0
            dma(q, xs[sx](), x_d[rr + blk * 128: rr + (blk + 1) * 128, :], xsem[sx] if q == "sp" else "xp%d" % sx)

        def p0_act(blk):
            sx = blk % NXS
            c = blk % 16
            act(junk0(), xs[sx](), AF.Square, accum=ssb((c, c + 1)))
            act(rsb((c, c + 1)), ssb((c, c + 1)), AF.Sqrt, bias=EPS, scale=1.0 / D)

        def p0_dve(blk):
            sx = blk % NXS
            s = blk % 2
            c = blk % 16
            recip(rsb((c, c + 1)), rsb((c, c + 1)))
            stt(xn[s](), xs[sx](), rsb((c, c + 1)), g1bc(), ALU.mult, ALU.mult)

        def p0_pe(blk):
            s = blk % 2
            g = next_pg()
            for k in range(16):
                tr(PGT[g](k), xn[s]((k * 128, (k + 1) * 128)), ident)
            copy(uT(None, (blk * 128, (blk + 1) * 128)), PGT[g](), "act" if blk % 2 == 0 else "dve")

        for it in range(NKB + 2):
            if it < NKB:
                if not (ti > 0 and it < NXS):
                    p0_load(it)
                p0_act(it)
                if it == 7:
                    prefetch_win(0, reads=[xs[it % NXS]()])
                if it == 10:
                    prefetch_win(1, reads=[xs[it % NXS]()])
            if 1 <= it <= NKB:
                p0_dve(it - 1)
            if it >= 2:
                p0_pe(it - 2)
        dma("sp", rope(), rope_d[ti].rearrange("p (a b) -> p a b", a=2), "tab")
        if dbg and ti == 0:
            dma_out("sp", dbg_d["uT"][:, :], uT(), "dbg")

        def u_rhs(k, a, b):
            return uT(k, (R0 + a, R0 + b))

        def win_chunk(ci):
            tno = ci // 2
            if tno not in wtiles:
                prefetch_win(tno)
            w = wtiles[tno]
            half = ci % 2
            return lambda k, w=w, half=half: w(k, (half * 128, half * 128 + 128))

        def rope_apply(src_ps, n, col0, dst_fn, t1, t2):
            cosv = lambda p: rope(0, (col0, col0 + n), p=p)
            sinv = lambda p: rope(1, (col0, col0 + n), p=p)
            tt(t1((0, n)), src_ps((0, 128)), cosv((0, 128)), ALU.mult)
            for q in range(4):
                sp_ = (q * 32, q * 32 + 32)
                dq = q ^ 1
                dp_ = (dq * 32, dq * 32 + 32)
                tt(t2((0, n), p=dp_), src_ps(sp_), sinv(sp_), ALU.mult)
            if dst_fn is not None:
                tt(dst_fn((0, 128)), t1((0, n)), t2((0, n)), ALU.add)

        ci = 0
        for qc in range(8):
            g = next_pg()
            proj(win_chunk(ci), 16, u_rhs, RN, g); ci += 1
            s = qc % 2
            rope_apply(lambda p, g=g: PG[g]((0, RN), p=p), RN, R0, lambda p, qc=qc: qT(qc, None, p=p), tmpA[2 * s], tmpA[2 * s + 1])
        for kc in range(2):
            g = next_pg()
            proj(win_chunk(ci), 16, lambda k, a, b: uT(k, (a, b)), KVN, g); ci += 1
            s = kc % 2
            kdst = {}
            rope_apply(lambda p, g=g: PG[g]((0, KVN), p=p), KVN, 0, None, tmpK[2 * s], tmpK[2 * s + 1])
            for half in range(2):
                sp_ = (half * 64, half * 64 + 64)
                for dp_ in ((0, 64), (64, 128)):
                    tt(kT(2 * kc + half, None, p=dp_), tmpK[2 * s]((0, KVN), p=sp_), tmpK[2 * s + 1]((0, KVN), p=sp_), ALU.add)
        if dbg and ti == 0:
            dma_out("sp", dbg_d["qT"][:, :], qT(), "dbg")
            dma_out("sp", dbg_d["kT"][:, :], kT(), "dbg")

        memset(vA(None, None, (64, 128)), 1.0)
        assert ci == 10
        prefetch_win(5)
        wv_ = wtiles[5]
        ci += 2
        for blk in range(NKB):
            b_ = next_pb()
            for k in range(16):
                mm(PB[b_]((0, 256)), uT(k, (blk * 128, (blk + 1) * 128)), wv_(k), k == 0, k == 15)
            pv = Buf(ps, "ps", PB[b_].off, F32, (4, 64))
            copy(vA(blk, None, (0, 64)), pv(), "act")
        if dbg and ti == 0:
            dma_out("sp", dbg_d["v"][:, :], vA(), "dbg")

        cC_sb, ccv, ccc = tmpA[0], tmpA[1], tmpA[2]
        den = tmpA[3]
        o_sb = tmpA[4]
        pt_base = tmpA[5].off
        ptl = [SB(pt_base + i * 768, BF16, (384,)) for i in range(3)]

        def conv_gen():
            cci = 12
            for c in range(8):
                w1 = par((PC_CONVA + 3 * c + 1, PC_CONVA + 3 * c + 2))
                w0 = par((PC_CONVA + 3 * c + 0, PC_CONVA + 3 * c + 1))
                w2 = par((PC_CONVA + 3 * c + 2, PC_CONVA + 3 * c + 3))
                for which in range(3):
                    wv = win_chunk(cci)
                    cci += 1
                    pcs = pieces(RN)
                    for k in range(16):
                        for (a, b) in pcs:
                            mm(PG[0]((a, b)), wv(k), u_rhs(k, a, b), k == 0, k == 15)
                        yield "k"
                    if which == 0:
                        copy(cC_sb(), PG[0]((0, RN)), "act")
                    elif which == 1:
                        tt(ccv(), PG[0]((0, RN)), cC_sb(), ALU.mult)
                        cpending.append(lambda w1=w1: act(ccc((1, RN - 1)), ccv((1, RN - 1)), AF.Identity, scale=w1))
                        cpending.append(lambda w0=w0: stt(ccc((1, RN - 1)), ccv((0, RN - 2)), w0, ccc((1, RN - 1)), ALU.mult, ALU.add))
                        cpending.append(lambda w2=w2: stt(ccc((1, RN - 1)), ccv((2, RN)), w2, ccc((1, RN - 1)), ALU.mult, ALU.add))
                    else:
                        while cpending:
                            cpending.pop(0)()
                        tt(ya(c, (1, RN - 1)), PG[0]((1, RN - 1)), ccc((1, RN - 1)), ALU.mult)
                    yield "end"

        MQ0, MQ1 = R0 + 1, R0 + RN - 1

        def attn_gen():
            for h in range(NH):
                kvh = h // 4
                qc = h // 2
                po = (h % 2) * 64
                pr = (po, po + 64)
                Ops_ = PG[1]
                steps = []
                for kb in range(NKB):
                    a = max((kb - 1) * 128, MQ0)
                    b = min((kb + 2) * 128, MQ1)
                    if b > a:
                        steps.append((kb, a, b))
                sb_of = {}
                touched = set()

                def qk(step, si):
                    kb, a, b = step
                    bnk = next_pb()
                    sb_of[si] = bnk
                    mm(PB[bnk]((0, b - a)), kT(kvh, (kb * 128, (kb + 1) * 128), p=pr), qT(qc, (a - R0, b - R0), p=pr), True, False, skip=True)
                    for nb in (kb - 1, kb + 1):
                        a2 = max(nb * 128, a)
                        b2 = min((nb + 1) * 128, b)
                        if b2 <= a2:
                            continue
                        i0, i1 = a2 - nb * 128, b2 - nb * 128
                        mb = mask_le(i0, i1) if nb == kb - 1 else mask_ge(i0, i1)
                        mm(PB[bnk]((a2 - a, b2 - a)), ident, mb, False, False, skip=True)

                def expo(step, si):
                    kb, a, b = step
                    n = b - a
                    act(ptl[si % 3]((0, n)), PB[sb_of[si]]((0, n)), AF.Exp, bias=kbias((kb, kb + 1)), scale=0.125)

                def pv(step, si):
                    kb, a, b = step
                    pt = ptl[si % 3]
                    for nb in (kb - 1, kb, kb + 1):
                        a2 = max(nb * 128, a)
                        b2 = min((nb + 1) * 128, b)
                        if b2 <= a2:
                            continue
                        bank = a2 // 512
                        first = bank not in touched
                        touched.add(bank)
                        mm(Ops_((a2, b2)), vA(kb, kvh, None), pt((a2 - a, b2 - a)), first, False, skip=True)

                qk(steps[0], 0)
                expo(steps[0], 0)
                for si, st in enumerate(steps):
                    if si + 1 < len(steps):
                        qk(steps[si + 1], si + 1)
                        expo(steps[si + 1], si + 1)
                    yield "A"
                    pv(st, si)
                    yield "B"
                nq = MQ1 - MQ0
                while pending:
                    pending.pop(0)()
                copy(o_sb((0, nq), p=(0, 64)), Ops_((MQ0, MQ1), p=(0, 64)), "act")
                ts(den((0, nq), p=(0, 64)), Ops_((MQ0, MQ1), p=(64, 128)), esink((h, h + 1), p=(64, 128)), ALU.add)
                NPC = 4
                step_ = (nq + NPC - 1) // NPC
                for i in range(NPC):
                    c0, c1 = i * step_, min(nq, (i + 1) * step_)
                    pending.append(lambda c0=c0, c1=c1: recip(den((c0, c1), p=(0, 64)), den((c0, c1), p=(0, 64))))
                pending.append(lambda qc=qc, pr=pr, nq=nq: tt(att(qc, (1, RN - 1), p=pr), o_sb((0, nq), p=(0, 64)), den((0, nq), p=(0, 64)), ALU.mult))
                yield "E"
            while pending:
                pending.pop(0)()
                yield "A"

        pending = []
        cpending = []
        ag = attn_gen()
        a_alive = [True]
        last_tag = [None]

        def adv_ag():
            if a_alive[0]:
                try:
                    last_tag[0] = next(ag)
                except StopIteration:
                    a_alive[0] = False
                    last_tag[0] = None
            return a_alive[0]

        kcount = 0
        debt = 0
        BURST = 10
        for tag in conv_gen():
            if tag == "k":
                kcount += 1
                if kcount % 2 == 0:
                    adv_ag()
                else:
                    if cpending:
                        cpending.pop(0)()
                    elif pending:
                        pending.pop(0)()
                    if debt > 0:
                        adv_ag()
                        debt -= 1
            else:
                for i in range(BURST):
                    adv_ag()
                    if last_tag[0] == "E":
                        debt += BURST - 1 - i
                        break
        while adv_ag():
            pass
        if dbg and ti == 0:
            dma_out("sp", dbg_d["ya"][:, :], ya(), "dbg")
            dma_out("sp", dbg_d["att"][:, :], att(), "dbg")

        for f in range(16):
            s = f % 2
            gs_a, gs_b, mtmp = tmpA[3 * s], tmpA[3 * s + 1], tmpA[3 * s + 2]
            slot = wload("M", win_d[18 + f])
            wg = wview(slot, (16, 256))
            if f == 4:
                dma("pool", wm0a(), wmix_d[0][:, 0:4096], "wm0a")
                dma("pool", wm0b(), wmix_d[0][:, 4096:8192], "wm0b")
            if f % 2 == 0:
                slot2 = wload("M", woo_d[f // 2])
                wo = wview(slot2, (2, 16, 128))
            fl = f % 2
            NM = RN - 2
            um = lambda k, a, b: uT(k, (R0 + 1 + a, R0 + 1 + b))
            g = next_pg()
            proj(lambda k: wg(k, (0, 128)), 16, um, NM, g)
            act(gs_a((0, NM)), PG[g]((0, NM)), AF.Sigmoid, bias=par((PC_BGATE + f, PC_BGATE + f + 1)))
            g = next_pg()
            proj(lambda k: wo(fl, k, None), 8, lambda k, a, b: ya(k, (1 + a, 1 + b)), NM, g)
            tt(mtmp((0, NM)), PG[g]((0, NM)), gs_a((0, NM)), ALU.mult)
            g = next_pg()
            proj(lambda k: wg(k, (128, 256)), 16, um, NM, g)
            act(gs_b((0, NM)), PG[g]((0, NM)), AF.Sigmoid, bias=par((PC_BGATE + 16 + f, PC_BGATE + 16 + f + 1)))
            g = next_pg()
            proj(lambda k: wo(fl, 8 + k, None), 8, lambda k, a, b: att(k, (1 + a, 1 + b)), NM, g)
            tt(gs_b((0, NM)), PG[g]((0, NM)), gs_b((0, NM)), ALU.mult)
            tt(mT(f, (1, RN - 1)), mtmp((0, NM)), gs_b((0, NM)), ALU.add)
        if dbg and ti == 0:
            dma_out("sp", dbg_d["mT"][:, :], mT(), "dbg")

        for ob in range(8):
            r = row0 + HALO + ob * 128
            dma("sp", hbuf(ob, None), x_d[r:r + 128, :], "hx%d" % ob)
        dma("sp", hx(p=(0, 1)), x_d[row0 + HALO - 1: row0 + HALO, :], "hxe0")
        dma("sp", hx(p=(1, 2)), x_d[row0 + HALO + T: row0 + HALO + T + 1, :], "hxe1")
        dma("sp", g2bc(), gvec_d[1, :].partition_broadcast(128), "g2")

        def p7_src(blk):
            if blk < 8:
                return (0, 128), hbuf(blk, None)
            return (0, 2), hx(p=(0, 2))

        def p7_act(blk):
            pp, src = p7_src(blk)
            c = blk
            act(junk(p=pp), src, AF.Square, accum=ssb((c, c + 1), p=pp))
            act(rsb((c, c + 1), p=pp), ssb((c, c + 1), p=pp), AF.Sqrt, bias=EPS, scale=1.0 / D)

        def p7_dve(blk):
            pp, src = p7_src(blk)
            c = blk
            s3 = blk % 3
            recip(rsb((c, c + 1), p=pp), rsb((c, c + 1), p=pp))
            if blk == 8:
                tt(rsb((c, c + 1), p=pp), rsb((c, c + 1), p=pp), evalid(p=pp), ALU.mult)
            stt(u2n[s3](p=pp), src, rsb((c, c + 1), p=pp), g2bc(p=pp), ALU.mult, ALU.mult)

        def p7_pe(blk):
            s3 = blk % 3
            g = next_pg()
            if blk < 8:
                for k in range(16):
                    tr(PGT[g](k), u2n[s3]((k * 128, (k + 1) * 128)), ident)
                copy(u2T(None, (2 + blk * 128, 2 + (blk + 1) * 128)), PGT[g](), "act" if blk % 2 == 0 else "dve")
            else:
                pp = (0, 2)
                for k in range(16):
                    tr(PGT[g](k, (0, 2)), u2n[s3]((k * 128, (k + 1) * 128), p=pp), cm((0, 2), p=pp))
                copy(u2T(None, (1, RN - 1, RN - 3)), PGT[g](None, (0, 2)), "dve")

        for fs in range(4):
            if fs == 0:
                wmk = lambda k: (wm0a(k, None) if k < 8 else wm0b(k - 8, None))
            else:
                slot = wload("F", wmix_d[fs])
                wm = wview(slot, (16, 512))
                wmk = lambda k, wm=wm: wm(k, None)
            for tb in range(9):
                b_ = next_pb3()
                for k in range(16):
                    if tb < 8:
                        lhs = mT(k, (2 + tb * 128, 2 + (tb + 1) * 128))
                        o = PB3[b_]((0, 512))
                    else:
                        lhs = mT(k, (1, RN - 1, RN - 3))
                        o = PB3[b_]((0, 512), p=(0, 2))
                    mm(o, lhs, wmk(k), k == 0, k == 15)
                if tb < 8:
                    tt(hbuf(tb, (fs * 512, (fs + 1) * 512)), PB3[b_]((0, 512)), hbuf(tb, (fs * 512, (fs + 1) * 512)), ALU.add)
                else:
                    tt(hx((fs * 512, (fs + 1) * 512), p=(0, 2)), PB3[b_]((0, 512), p=(0, 2)), hx((fs * 512, (fs + 1) * 512), p=(0, 2)), ALU.add)
                if fs == 3 and PIPE7:
                    p7_act(tb)
                    if tb >= 1:
                        p7_dve(tb - 1)
                    if tb >= 3:
                        p7_pe(tb - 3)
        if PIPE7:
            p7_dve(8)
            for blk in (6, 7, 8):
                p7_pe(blk)
        else:
            for blk in range(9):
                p7_act(blk)
                p7_dve(blk)
                p7_pe(blk)
        if dbg and ti == 0:
            dma_out("sp", dbg_d["h1"][:, :], hbuf(), "dbg")
            dma_out("sp", dbg_d["u2T"][:, :], u2T(), "dbg")

        def ffn_up(gi):
            slot_a = wload("F", wup_d[2 * gi])
            wa = wview(slot_a, (16, 512))
            slot_g = wload("F", wup_d[2 * gi + 1])
            wgv = wview(slot_g, (16, 512))
            for j in range(GRP):
                for which in range(2):
                    w = wa if which == 0 else wgv
                    chunk = (gi * GRP + j) + (0 if which == 0 else NFF)
                    pb_ = psb[which]
                    if FFN_XB:
                        g = ug_i[0] % 2
                        ug_i[0] += 1
                        xslot = xb_i[0] % 4
                        xb_i[0] += 1
                        xv = XB((2 * xslot, 2 * xslot + 2))
                        for k in range(16):
                            wk = w(k, (j * 128, (j + 1) * 128))
                            mm(UG[g]((0, 512)), wk, u2T(k, (2, 514)), k == 0, k == 15)
                            mm(UG[g]((512, 1024)), wk, u2T(k, (514, 1026)), k == 0, k == 15)
                            mm(xv, wk, u2T(k, (1, RN - 1, RN - 3)), k == 0, k == 15, skip=True)
                        copy(pb_((1, T + 1)), UG[g]((0, T)), "act")
                        copy(pb_((0, FN, FN - 1)), xv, "act")
                    else:
                        g = next_pg()
                        proj(lambda k, w=w, j=j: w(k, (j * 128, (j + 1) * 128)), 16, lambda k, a, b: u2T(k, (1 + a, 1 + b)), FN, g)
                        copy(pb_((0, FN)), PG[g]((0, FN)), "act")
                    dst = (a_c if which == 0 else gv_c)[j % 2]
                    w0 = par((PC_FCW + 3 * chunk, PC_FCW + 3 * chunk + 1))
                    w1 = par((PC_FCW + 3 * chunk + 1, PC_FCW + 3 * chunk + 2))
                    w2 = par((PC_FCW + 3 * chunk + 2, PC_FCW + 3 * chunk + 3))
                    bb = par((PC_FCB + chunk, PC_FCB + chunk + 1))
                    act(dst(), pb_((1, T + 1)), AF.Identity, bias=bb, scale=w1)
                    stt(dst(), pb_((0, T)), w0, dst(), ALU.mult, ALU.add)
                    stt(dst(), pb_((2, T + 2)), w2, dst(), ALU.mult, ALU.add)
                act(sil[0](), a_c[j % 2](), AF.Silu)
                tt(actT[gi % 2](j, None), sil[0](), gv_c[j % 2](), ALU.mult)

        def p9_act(ob):
            c = ob
            act(junk(), hbuf(ob, None), AF.Square, accum=ssb((c, c + 1)))
            act(rsb((c, c + 1)), ssb((c, c + 1)), AF.Sqrt, bias=EPS, scale=1.0 / D)

        def p9_dve(ob):
            c = ob
            s2 = ob % 2
            recip(rsb((c, c + 1)), rsb((c, c + 1)))
            stt(ost[s2](), hbuf(ob, None), rsb((c, c + 1)), gFbc(), ALU.mult, ALU.mult)
            r = ti * T + ob * 128
            dma_out("sp", out_d[r:r + 128, :], ost[s2](), "ost%d" % s2)

        def ffn_down(gi, last):
            slot = wload("F", wdn_d[gi])
            wd = wview(slot, (GRP, D))
            for tb in range(8):
                for fs in range(4):
                    if FFN_XB:
                        bank = PB3[next_pb3()]
                    else:
                        bank = DB4[db4_i[0] % 4]
                        db4_i[0] += 1
                    for kk in range(GRP):
                        mm(bank((0, 512)), actT[gi % 2](kk, (tb * 128, (tb + 1) * 128)), wd(kk, (fs * 512, (fs + 1) * 512)), kk == 0, kk == GRP - 1)
                    tt(hbuf(tb, (fs * 512, (fs + 1) * 512)), bank((0, 512)), hbuf(tb, (fs * 512, (fs + 1) * 512)), ALU.add)
                if last and PIPE9:
                    p9_act(tb)
                    if tb >= 1:
                        p9_dve(tb - 1)
            if last and PIPE9:
                p9_dve(7)
            if last and not PIPE9:
                for ob in range(8):
                    p9_act(ob)
                    p9_dve(ob)

        ffn_up(0)
        for gi in range(NGRP):
            if gi + 1 < NGRP:
                ffn_up(gi + 1)
            if gi == NGRP - 1:
                dma("sp", gFbc(), gvec_d[2, :].partition_broadcast(128), "gF")
            ffn_down(gi, gi == NGRP - 1)
            if gi == NGRP - 2 and ti + 1 < NT:
                dma("pool", g1bc(), gvec_d[0, :].partition_broadcast(128), "g1p")
                for b0 in range(NXS):
                    p0_load(b0, q="pool", r0=(ti + 1) * T)

    P.emit(nc, es)
    es.close()
    return nc


def _prep_shared(inp):
    f32 = np.float32
    w_in = np.asarray(inp["w_in"], dtype=f32)[0]
    chunk_cols = []
    for qc in range(8):
        chunk_cols.append(np.arange(3072 + 128 * qc, 3072 + 128 * qc + 128))
    for kc in range(2):
        chunk_cols.append(np.arange(4096 + 128 * kc, 4096 + 128 * kc + 128))
    chunk_cols.append(np.arange(4352, 4352 + 128))
    chunk_cols.append(np.arange(4352 + 128, 4352 + 256))
    for c in range(8):
        for c0 in (1024 + 128 * c, 2048 + 128 * c, 128 * c):
            chunk_cols.append(np.arange(c0, c0 + 128))
    for f in range(16):
        chunk_cols.append(np.arange(4608 + 128 * f, 4608 + 128 * f + 128))
        chunk_cols.append(np.arange(6656 + 128 * f, 6656 + 128 * f + 128))
    assert len(chunk_cols) == 68
    allc = np.concatenate(chunk_cols)
    wsel = w_in[:, allc]
    w_in_t = np.ascontiguousarray(wsel.reshape(16, 128, 34, 256).transpose(2, 1, 0, 3)).reshape(34, 128, 4096)

    w_oa = np.asarray(inp["w_out_a"], dtype=f32)[0]
    w_ob = np.asarray(inp["w_o_attn"], dtype=f32)[0]
    wcat = np.concatenate([w_oa, w_ob], axis=0)
    w_oo_t = np.ascontiguousarray(wcat.reshape(16, 128, 8, 2, 128).transpose(2, 1, 3, 0, 4)).reshape(8, 128, 4096)

    w_mix = np.asarray(inp["w_mix_out"], dtype=f32)[0]
    w_mix_t = np.ascontiguousarray(w_mix.reshape(16, 128, 4, 512).transpose(2, 1, 0, 3)).reshape(4, 128, 8192)

    w_up = np.asarray(inp["ffn_w_up"], dtype=f32)[0]
    wu = w_up.reshape(16, 128, 2, NGRP, 512)
    w_up_t = np.ascontiguousarray(wu.transpose(3, 2, 1, 0, 4)).reshape(2 * NGRP, 128, 8192)

    w_dn = np.asarray(inp["ffn_w_down"], dtype=f32)[0]
    wd = w_dn.reshape(NGRP, GRP, 128, D)
    w_dn_t = np.ascontiguousarray(wd.transpose(0, 2, 1, 3)).reshape(NGRP, 128, 8192)

    params = np.zeros((128, NPAR), dtype=f32)
    ca = np.asarray(inp["conv_a_w"], dtype=f32)[0]
    params[:, PC_CONVA:PC_CONVA + 24] = ca.reshape(3, 8, 128).transpose(2, 1, 0).reshape(128, 24)
    bg = np.asarray(inp["b_gate"], dtype=f32)[0]
    params[:, PC_BGATE:PC_BGATE + 32] = bg.reshape(32, 128).T
    fw = np.asarray(inp["ffn_conv_w"], dtype=f32)[0]
    params[:, PC_FCW:PC_FCW + 264] = fw.reshape(3, 88, 128).transpose(2, 1, 0).reshape(128, 264)
    fb = np.asarray(inp["ffn_conv_b"], dtype=f32)[0]
    params[:, PC_FCB:PC_FCB + 88] = fb.reshape(88, 128).T
    sk = np.asarray(inp["sink_logits"], dtype=f32)[0]
    params[:, PC_SINK:PC_SINK + 16] = np.broadcast_to(sk[None, :], (128, 16))

    gvec = np.stack([np.asarray(inp["norm_mix_g"], dtype=f32)[0],
                     np.asarray(inp["norm_ffn_g"], dtype=f32)[0],
                     np.asarray(inp["norm_final_g"], dtype=f32)], axis=0)

    j = np.arange(128)
    cmask = np.zeros((128, 512), dtype=f32)
    cmask[:, 0:128] = np.eye(128, dtype=f32)
    if MASK_PE:
        cmask[:, 128:256] = np.where(j[:, None] <= j[None, :], 0.0, -30000.0).astype(f32)
        cmask[:, 384:512] = np.where(j[:, None] >= j[None, :], 0.0, -30000.0).astype(f32)
    else:
        cmask[:, 128:256] = (j[:, None] <= j[None, :]).astype(f32)
        cmask[:, 384:512] = (j[:, None] >= j[None, :]).astype(f32)
    return dict(w_in_t=w_in_t, w_oo_t=w_oo_t, w_mix_t=w_mix_t, w_up_t=w_up_t, w_dn_t=w_dn_t,
                params=params, gvec=gvec, cmask=cmask)


def _prep_core(x2d_pad, c):
    f32 = np.float32
    base = c * TOKC
    xs = np.ascontiguousarray(x2d_pad[base: base + XROWS])
    half = HD // 2
    inv_freq = 10000.0 ** (-(np.arange(0, half, dtype=np.float64) / float(half)))
    p = np.arange(128)
    fidx = p % 32
    sign = np.where((p % 64) < 32, 1.0, -1.0).astype(f32)
    rope = np.zeros((NT, 128, 2 * KVN), dtype=f32)
    kbias = np.zeros((NT, 128, NKB), dtype=f32)
    evalid = np.zeros((NT, 2, 1), dtype=f32)
    for ti in range(NT):
        s = base + ti * T
        pos = (s - HALO + np.arange(KVN)).astype(np.int64)
        posf = np.clip(pos, 0, SEQ - 1).astype(np.float64)
        ang = posf[None, :] * inv_freq[fidx][:, None]
        rope[ti, :, 0:KVN] = np.cos(ang).astype(f32)
        rope[ti, :, KVN:] = (np.sin(ang).astype(f32) * sign[:, None]).astype(f32)
        kpos = (s - HALO) + np.arange(NKB)[None, :] * 128 + p[:, None]
        kbias[ti] = np.where((kpos >= 0) & (kpos < SEQ), 0.0, -30000.0).astype(f32)
        evalid[ti, 0, 0] = 1.0 if (s - 1) >= 0 else 0.0
        evalid[ti, 1, 0] = 1.0 if (s + T) < SEQ else 0.0
    return dict(x=xs, rope=rope, kbias=kbias, evalid=evalid)


_NC_CACHE = {}


def kernel(**inputs):
    x = np.asarray(inputs["x"], dtype=np.float32)
    x2d = x.reshape(SEQ, D)
    xpad = np.zeros((SEQ + 2 * HALO, D), dtype=np.float32)
    xpad[HALO:HALO + SEQ] = x2d
    shared = _prep_shared(inputs)
    in_maps = []
    for c in range(NCORE):
        m = dict(shared)
        m.update(_prep_core(xpad, c))
        in_maps.append(m)
    if "nc" not in _NC_CACHE:
        _NC_CACHE["nc"] = build_program(DEBUG)
    nc = _NC_CACHE["nc"]
    res = run_bass_kernel_spmd(nc, in_maps, core_ids=list(range(NCORE)))
    if DEBUG:
        _NC_CACHE["last"] = res
    out = np.concatenate([np.asarray(r["out"], dtype=np.float32) for r in res.results], axis=0)
    return out.reshape(1, SEQ, D)
```
